# Optimizing a Trainium2 kernel written in Bass

```python
import jax, jax.numpy as jnp
from jax import lax
import numpy as np

D_MODEL = 2048
BATCH = 4
SEQ = 4096
DEPTH = 1

CHUNK = 64
N_LEFT_CHUNKS = 8
N_BAND = N_LEFT_CHUNKS + 1
A_HEADS = 8
A_WIDTH = D_MODEL // 2
A_HEAD_DIM = A_WIDTH // A_HEADS
REL_CLIP = 256
POOL_WINDOWS = (2, 4, 8, 16)
POOL_GROUPS = len(POOL_WINDOWS)
POOL_WIDTH = D_MODEL // 4
POOL_GROUP_DIM = POOL_WIDTH // POOL_GROUPS
M_HEADS = 4
M_WIDTH = D_MODEL // 4
M_HEAD_DIM = M_WIDTH // M_HEADS
N_MEM = 256
N_BRANCH = 3
IN_SPLITS = (A_WIDTH, A_WIDTH, A_WIDTH, A_WIDTH, POOL_WIDTH, POOL_WIDTH, M_WIDTH, M_WIDTH)
IN_WIDTH = sum(IN_SPLITS)
EPS = 1e-6
NEG_INF = -1e30

kernel_name = "hybrid_chunk_attn_pool_mem_block"


def _rmsnorm(t, gain):
    tf = t.astype(jnp.float32)
    tf = tf * lax.rsqrt(jnp.mean(tf * tf, axis=-1, keepdims=True) + EPS)
    return (tf * gain.astype(jnp.float32)).astype(t.dtype)


def _head_rmsnorm(t, gain):
    tf = t.astype(jnp.float32)
    tf = tf * lax.rsqrt(jnp.mean(tf * tf, axis=-1, keepdims=True) + EPS)
    return (tf * gain.astype(jnp.float32)).astype(t.dtype)


def _chunk_band_attention(q, k, v, q_gain, k_gain, rel_bias):
    B, S, H, Dh = q.shape
    nc = S // CHUNK
    q = _head_rmsnorm(q, q_gain).reshape(B, nc, CHUNK, H, Dh)
    k = _head_rmsnorm(k, k_gain)
    pad = jnp.zeros((B, N_LEFT_CHUNKS * CHUNK, H, Dh), k.dtype)
    kp = jnp.concatenate([pad, k], axis=1).reshape(B, nc + N_LEFT_CHUNKS, CHUNK, H, Dh)
    vp = jnp.concatenate([pad.astype(v.dtype), v], axis=1).reshape(B, nc + N_LEFT_CHUNKS, CHUNK, H, Dh)
    band_idx = np.arange(nc)[:, None] + np.arange(N_BAND)[None, :]
    kb = kp[:, band_idx].reshape(B, nc, N_BAND * CHUNK, H, Dh)
    vb = vp[:, band_idx].reshape(B, nc, N_BAND * CHUNK, H, Dh)
    scores = jnp.einsum('bnqhd,bnkhd->bnhqk', q, kb).astype(jnp.float32) * (Dh ** -0.5)
    i = np.arange(CHUNK)
    s_off = np.arange(N_BAND)
    dist = ((N_LEFT_CHUNKS - s_off)[None, :, None] * CHUNK
            + i[:, None, None] - i[None, None, :]).reshape(CHUNK, N_BAND * CHUNK)
    bias_idx = np.clip(dist, -REL_CLIP, REL_CLIP) + REL_CLIP
    bias = rel_bias.astype(jnp.float32)[:, bias_idx]
    valid = (np.arange(nc)[:, None] + s_off[None, :] - N_LEFT_CHUNKS) >= 0
    valid = np.repeat(valid, CHUNK, axis=1)
    scores = scores + bias[None, None]
    scores = jnp.where(valid[None, :, None, None, :], scores, NEG_INF)
    probs = jax.nn.softmax(scores, axis=-1).astype(v.dtype)
    out = jnp.einsum('bnhqk,bnkhd->bnqhd', probs, vb)
    return out.reshape(B, S, H * Dh)


def _multiscale_pool(v, pool_w, pool_scale):
    B, S, P = v.shape
    vg = v.reshape(B, S, POOL_GROUPS, POOL_GROUP_DIM).astype(jnp.float32)
    cs = jnp.concatenate([jnp.zeros((B, 1, POOL_GROUPS, POOL_GROUP_DIM), jnp.float32),
                          jnp.cumsum(vg, axis=1)], axis=1)
    t = np.arange(S)
    outs = []
    for g, w in enumerate(POOL_WINDOWS):
        lo = np.maximum(t + 1 - w, 0)
        cnt = (t + 1 - lo).astype(np.float32)
        mean = (cs[:, 1:, g] - cs[:, lo, g]) / cnt[None, :, None]
        outs.append(mean - vg[:, :, g])
    pooled = jnp.stack(outs, axis=2).astype(v.dtype)
    mixed = jnp.einsum('bsgc,gcd->bsgd', pooled, pool_w).reshape(B, S, P)
    return mixed * pool_scale


def _memory_attention(q, mk, mv, q_gain, k_gain):
    B, S, H, Dh = q.shape
    q = _head_rmsnorm(q, q_gain)
    mk = _head_rmsnorm(mk, k_gain)
    scores = jnp.einsum('bshd,bnhd->bhsn', q, mk).astype(jnp.float32) * (Dh ** -0.5)
    probs = jax.nn.softmax(scores, axis=-1).astype(mv.dtype)
    out = jnp.einsum('bhsn,bnhd->bshd', probs, mv)
    return out.reshape(B, S, H * Dh)


def setup_inputs(seed: int = 0) -> dict:
    key = jax.random.key(seed)
    ks = jax.random.split(key, 20)
    L, D = DEPTH, D_MODEL
    nrm = jax.random.normal
    f32 = jnp.float32
    return {
        "x": nrm(ks[0], (BATCH, SEQ, D), f32),
        "mem": nrm(ks[1], (BATCH, N_MEM, D), f32),
        "norm_gain": 1.0 + 0.1 * nrm(ks[2], (L, D), f32),
        "mem_norm_gain": 1.0 + 0.1 * nrm(ks[3], (L, D), f32),
        "w_in": nrm(ks[4], (L, D, IN_WIDTH), f32) * D ** -0.5,
        "w_merge": nrm(ks[5], (L, D, N_BRANCH * D), f32) * D ** -0.5,
        "b_merge": 0.1 * nrm(ks[6], (L, N_BRANCH * D), f32),
        "a_q_gain": 1.0 + 0.1 * nrm(ks[7], (L, A_HEADS, A_HEAD_DIM), f32),
        "a_k_gain": 1.0 + 0.1 * nrm(ks[8], (L, A_HEADS, A_HEAD_DIM), f32),
        "a_rel_bias": 0.5 * nrm(ks[9], (L, A_HEADS, 2 * REL_CLIP + 1), f32),
        "pool_w": nrm(ks[10], (L, POOL_GROUPS, POOL_GROUP_DIM, POOL_GROUP_DIM), f32) * POOL_GROUP_DIM ** -0.5,
        "pool_scale": 1.0 + 0.1 * nrm(ks[11], (L, POOL_WIDTH), f32),
        "w_mem_kv": nrm(ks[12], (L, D, 2 * M_WIDTH), f32) * D ** -0.5,
        "m_q_gain": 1.0 + 0.1 * nrm(ks[13], (L, M_HEADS, M_HEAD_DIM), f32),
        "m_k_gain": 1.0 + 0.1 * nrm(ks[14], (L, M_HEADS, M_HEAD_DIM), f32),
        "w_branch_a": nrm(ks[15], (L, A_WIDTH, D), f32) * A_WIDTH ** -0.5,
        "w_branch_b": nrm(ks[16], (L, POOL_WIDTH, D), f32) * POOL_WIDTH ** -0.5,
        "w_branch_m": nrm(ks[17], (L, M_WIDTH, D), f32) * M_WIDTH ** -0.5,
        "w_out": nrm(ks[18], (L, D, D), f32) * D ** -0.5,
    }


def reference(x, mem, norm_gain, mem_norm_gain, w_in, w_merge, b_merge, a_q_gain, a_k_gain,
              a_rel_bias, pool_w, pool_scale, w_mem_kv, m_q_gain, m_k_gain,
              w_branch_a, w_branch_b, w_branch_m, w_out):
    B, S, D = x.shape
    offsets = [int(o) for o in np.cumsum(IN_SPLITS)[:-1]]
    for l in range(DEPTH):
        h = _rmsnorm(x, norm_gain[l])
        proj = h @ w_in[l]
        qa, ka, va, ga, vb, gb, qm, gm = jnp.split(proj, offsets, axis=-1)
        gates = jax.nn.sigmoid(h @ w_merge[l] + b_merge[l]).reshape(B, S, N_BRANCH, D)

        o_a = _chunk_band_attention(qa.reshape(B, S, A_HEADS, A_HEAD_DIM),
                                    ka.reshape(B, S, A_HEADS, A_HEAD_DIM),
                                    va.reshape(B, S, A_HEADS, A_HEAD_DIM),
                                    a_q_gain[l], a_k_gain[l], a_rel_bias[l])
        o_a = o_a * jax.nn.silu(ga)

        o_b = _multiscale_pool(vb, pool_w[l], pool_scale[l]) * jax.nn.silu(gb)

        mem_h = _rmsnorm(mem, mem_norm_gain[l])
        mk, mv = jnp.split(mem_h @ w_mem_kv[l], 2, axis=-1)
        n_mem = mem.shape[1]
        o_m = _memory_attention(qm.reshape(B, S, M_HEADS, M_HEAD_DIM),
                                mk.reshape(B, n_mem, M_HEADS, M_HEAD_DIM),
                                mv.reshape(B, n_mem, M_HEADS, M_HEAD_DIM),
                                m_q_gain[l], m_k_gain[l])
        o_m = o_m * jax.nn.silu(gm)

        y = (gates[:, :, 0] * (o_a @ w_branch_a[l])
             + gates[:, :, 1] * (o_b @ w_branch_b[l])
             + gates[:, :, 2] * (o_m @ w_branch_m[l]))
        x = x + y @ w_out[l]
    return x
```

```python
import contextlib
import numpy as np
import concourse.bass as bass
import concourse.mybir as mybir
from concourse.bass_utils import run_bass_kernel_spmd

F32 = mybir.dt.float32
BF16 = mybir.dt.bfloat16
AF = mybir.ActivationFunctionType
ALU = mybir.AluOpType

D = 2048
T = 1024
HALO = 512
SL = T + HALO
NPASS = 2
NKC = 16
NSLOT = 3
EPS = 1e-6
NEG = -30000.0
SCALE = float(128 ** -0.5)
S_HI = float(np.float32(SCALE))
S_LO = float(np.float32(SCALE - S_HI))
SQ_SCALE = 2.0 ** -5
LN_SCALE = 0.5
assert abs(SQ_SCALE * SQ_SCALE * LN_SCALE * D - 1.0) < 1e-15
POOL_W = (2, 4, 8, 16)
N_CORES = 8


class _Op:
    __slots__ = ("eng", "fn", "dma_key", "needs_inc", "count", "waits", "idx")


class Prog:
    def __init__(self, nc):
        self.nc = nc
        self.eng = dict(pe=nc.tensor, act=nc.scalar, dve=nc.vector, pool=nc.gpsimd, sp=nc.sync)
        self.ops = []
        self.res = {}
        self.dma_cnt = {}

    def op(self, eng, fn, reads=(), writes=(), dma_key=None):
        reads = list(reads)
        writes = list(writes)
        if any(isinstance(k, tuple) and k[0] == "A" for k in reads + writes):
            reads.append("arena")
        if any(isinstance(k, tuple) and k[0] == "B" for k in reads + writes):
            reads.append("arena2")
        if any(isinstance(k, tuple) and k[0] == "C" for k in reads + writes):
            reads.append("arena3")
        o = _Op()
        o.eng = eng
        o.fn = fn()
        o.dma_key = dma_key
        o.needs_inc = False
        o.count = None
        o.idx = len(self.ops)
        deps = []
        for k in reads:
            r = self.res.get(k)
            if r is not None and r[0] is not None:
                deps.append(r[0])
        for k in writes:
            r = self.res.get(k)
            if r is not None:
                if r[0] is not None:
                    deps.append(r[0])
                deps.extend(r[1])
        for k in reads:
            r = self.res.get(k)
            if r is None:
                r = self.res[k] = [None, []]
            r[1].append(o)
        for k in writes:
            self.res[k] = [o, []]
        waits = {}
        for d in deps:
            if d is o:
                continue
            if d.dma_key is not None:
                key = ("dma", d.dma_key)
                v = 16 * self.dma_cnt[d.dma_key]
                if waits.get(key, (0, None))[0] < v:
                    waits[key] = (v, None)
            else:
                if d.eng == eng and eng == "pe":
                    continue
                d.needs_inc = True
                key = ("eng", d.eng)
                cur = waits.get(key)
                if cur is None or cur[1].idx < d.idx:
                    waits[key] = (None, d)
        o.waits = waits
        if dma_key is not None:
            self.dma_cnt[dma_key] = self.dma_cnt.get(dma_key, 0) + 1
        self.ops.append(o)
        return o

    def emit(self):
        nc = self.nc
        sems = {}
        with contextlib.ExitStack() as es:
            for e in self.eng:
                sems[("eng", e)] = es.enter_context(nc.semaphore("s_" + e))
            for i, k in enumerate(self.dma_cnt):
                sems[("dma", k)] = es.enter_context(nc.semaphore("d_%d" % i))
            cnt = {e: 0 for e in self.eng}
            waited = {e: {} for e in self.eng}
            for o in self.ops:
                E = self.eng[o.eng]
                for key, (v, d) in o.waits.items():
                    if d is not None:
                        v = d.count
                    if waited[o.eng].get(key, 0) >= v:
                        continue
                    waited[o.eng][key] = v
                    E.wait_ge(sems[key], v)
                m_, a_, k_ = o.fn
                ins = m_(*a_, **k_)
                if o.dma_key is not None:
                    ins.then_inc(sems[("dma", o.dma_key)], 16)
                elif o.needs_inc:
                    cnt[o.eng] += 1
                    o.count = cnt[o.eng]
                    ins.then_inc(sems[("eng", o.eng)], 1)
            E = self.eng["sp"]
            for k, n in self.dma_cnt.items():
                E.wait_ge(sems[("dma", k)], 16 * n)
            for e in self.eng:
                if cnt[e] > 0:
                    E.wait_ge(sems[("eng", e)], cnt[e])


class _Rec:
    def __init__(self, eng):
        self._e = eng

    def __getattr__(self, name):
        m = getattr(self._e, name)
        return lambda *a, **k: (m, a, k)


class Rot:
    def __init__(self, items):
        self.items = list(items)
        self.i = 0

    def next(self):
        v = self.items[self.i % len(self.items)]
        self.i += 1
        return v


def chunk_specs():
    pro = [("memk", [("w_mem_kv", 0, 2048, 0, 512, 0, 0)]),
           ("memv", [("w_mem_kv", 0, 2048, 512, 512, 0, 0)])]
    per = []
    for h in range(8):
        per.append(("head%d" % h, [("w_in", 0, 2048, j * 1024 + h * 128, 128, 0, j * 128) for j in range(4)]))
    per.append(("vb", [("w_in", 0, 2048, 4096, 512, 0, 0)]))
    per.append(("gb", [("w_in", 0, 2048, 4608, 512, 0, 0)]))
    per.append(("qm", [("w_in", 0, 2048, 5120, 512, 0, 0)]))
    per.append(("gm", [("w_in", 0, 2048, 5632, 512, 0, 0)]))
    for db in range(16):
        pcs = [("w_merge", 0, 2048, br * 2048 + db * 128, 128, 0, br * 128) for br in range(3)]
        pcs.append(("w_a", 0, 1024, db * 128, 128, 0, 384))
        pcs.append(("w_b", 0, 512, db * 128, 128, 1024, 384))
        pcs.append(("w_m", 0, 512, db * 128, 128, 1536, 384))
        per.append(("mg%d" % db, pcs))
    for c in range(4):
        per.append(("wo%d" % c, [("w_out", 0, 2048, c * 512, 512, 0, 0)]))
    return pro, per


def build_wstream(ws):
    pro, per = chunk_specs()
    allc = pro + per
    out = np.empty((len(allc), 128, NKC * 512), np.float32)
    for n, (tag, pcs) in enumerate(allc):
        m = np.empty((2048, 512), np.float32)
        for (src, r0, nr, c0, ncol, dr, dc) in pcs:
            m[dr:dr + nr, dc:dc + ncol] = ws[src][r0:r0 + nr, c0:c0 + ncol]
        out[n] = m.reshape(NKC, 128, 512).transpose(1, 0, 2).reshape(128, NKC * 512)
    return out


def build_program():
    nc = bass.Bass("TRN2", target_bir_lowering=False)

    def din(name, shape):
        return nc.dram_tensor(name, list(shape), F32, kind="ExternalInput").ap()

    xs = din("xs", [HALO + NPASS * T, D])
    mem = din("mem", [256, D])
    gain_d = din("gain_bc", [128, D])
    mgain_d = din("mgain_bc", [128, D])
    _pro, _per = chunk_specs()
    wstream_d = din("wstream", [len(_pro) + len(_per), 128, NKC * 512])
    vecs_d = din("vecs", [128, 76])
    relb_d = din("relb", [8, 128, 640])
    poolw_d = din("pool_w", [4, 128, 128])
    ident_d = din("ident", [128, 128])
    kmask_d = din("kmask", [128, 1])
    pcorr_d = din("pcorr", [128, 64])
    out_d = nc.dram_tensor("out", [NPASS * T, D], F32, kind="ExternalOutput").ap()
    kvs_k = nc.dram_tensor("kvs_k", [8, 128, HALO], BF16, kind="Internal").ap()
    kvs_v = nc.dram_tensor("kvs_v", [8, 128, HALO], BF16, kind="Internal").ap()

    es = contextlib.ExitStack()
    with es:
        def sb(name, shape, dt):
            return es.enter_context(nc.sbuf_tensor("sb_" + name, list(shape), dt))

        hT = sb("hT", [128, NKC, SL], BF16)
        o_flat = sb("o_sb", [128, 16 * T], BF16)
        o_sb = o_flat[:, :].rearrange("p (k t) -> p k t", k=16)
        yp_flat = sb("yp", [128, 16 * T], BF16)
        yp = yp_flat[:, :].rearrange("p (k t) -> p k t", k=16)
        memT = yp_flat[:, 0:NKC * 256].rearrange("p (k t) -> p k t", k=NKC)
        wsl = [sb("wsl%d" % i, [128, NKC, 512], BF16) for i in range(NSLOT)]
        vecs = sb("vecs", [128, 76], F32)
        bmh = sb("bmh", [128, 48], F32)
        gqs = sb("gqs", [128, 12], F32)
        gqt = sb("gqt", [128, 12], F32)
        psc = sb("psc", [128, 4], F32)
        ident_f = sb("ident_f", [128, 128], F32)
        ident_b = sb("ident_b", [128, 128], BF16)
        onesb = sb("onesb", [128, 128], BF16)
        ones1 = sb("ones1", [128, 128], BF16)
        ones2 = sb("ones2", [128, 128], BF16)
        epsb = sb("epsb", [128, 1], F32)
        kmask = sb("kmask", [128, 1], F32)
        pcorr = sb("pcorr", [128, 64], F32)
        poolw = sb("poolw", [128, 4, 128], BF16)
        mk_n = sb("mk_n", [128, 4, 256], BF16)
        mv = sb("mv", [128, 2, 512], BF16)
        Bh = [sb("Bh%d" % i, [128, 640], F32) for i in range(2)]
        stat = sb("stat", [128, 64], F32)
        dummy = sb("dummy", [128, 1], F32)
        dummy2 = sb("dummy2", [128, 1], F32)
        dummy3 = sb("dummy3", [128, 1], F32)
        AW = 8192
        arena = sb("arena", [128, AW], F32)
        ps = es.enter_context(nc.psum_tensor("ps", [128, 4096], F32))

        P = Prog(nc)
        V_ = _Rec(nc.vector)
        A_ = _Rec(nc.scalar)
        G_ = _Rec(nc.gpsimd)
        PE = _Rec(nc.tensor)
        SP = _Rec(nc.sync)

        arena2 = yp_flat[:, :].bitcast(F32)

        class Arena:
            def __init__(self, base=None):
                self.off = 0
                self.base = arena if base is None else base

            def f32(self, n):
                a = self.base[:, self.off:self.off + n]
                self.off += n
                assert self.off <= AW, self.off
                return a

            def bf16(self, n):
                assert n % 2 == 0
                a = self.base[:, self.off:self.off + n // 2].bitcast(BF16)
                self.off += n // 2
                assert self.off <= AW, self.off
                return a

        def fence():
            P.op("pool", lambda: G_.memset(dummy[:], 0.0), writes=["arena"])

        arena3 = o_flat[:, :].bitcast(F32)

        def fence3():
            P.op("pool", lambda: G_.memset(dummy3[:], 0.0),
                 writes=["arena3"] + [("o", b_, h_) for b_ in range(16) for h_ in range(2)])

        def fence2():
            P.op("pool", lambda: G_.memset(dummy2[:], 0.0), writes=["arena2"] + [("yp", i) for i in range(16)])

        def bank(i):
            return ps[:, i * 512:(i + 1) * 512]

        def pair(i):
            return ps[:, i * 512:(i + 2) * 512]

        chunks = []
        for p_ in range(NPASS):
            for i, (t, _) in enumerate(_per):
                chunks.append((t, len(_pro) + i))
                if p_ == 0 and t == "qm":
                    chunks += [(t2, i2) for i2, (t2, _) in enumerate(_pro)]
        wstate = dict(cur=-1, issued=0)

        def issue_chunk(n):
            s = n % NSLOT
            idx = chunks[n][1]
            P.op("pool", lambda: G_.dma_start(out=wsl[s][:], in_=wstream_d[idx].rearrange("p (k c) -> p k c", k=NKC)),
                 reads=([("hT", 1)] if n < NSLOT else []), writes=[("w", s)], dma_key=("w", s))

        def wnext(tag):
            wstate["cur"] += 1
            n = wstate["cur"]
            assert chunks[n][0] == tag, (chunks[n][0], tag)
            while wstate["issued"] < min(len(chunks), n + NSLOT):
                issue_chunk(wstate["issued"])
                wstate["issued"] += 1
            s = n % NSLOT
            return wsl[s], ("w", s)

        P.op("sp", lambda: SP.dma_start(out=vecs[:], in_=vecs_d[:, :]), writes=["vecs"], dma_key="c")
        P.op("sp", lambda: SP.dma_start(out=ident_f[:], in_=ident_d[:, :]), writes=["ident_f"], dma_key="c")
        P.op("sp", lambda: SP.dma_start(out=kmask[:], in_=kmask_d[:, :]), writes=["kmask"], dma_key="c")
        P.op("sp", lambda: SP.dma_start(out=pcorr[:], in_=pcorr_d[:, :]), writes=["pcorr"], dma_key="c")
        P.op("pool", lambda: G_.dma_start(out=poolw[:], in_=poolw_d.rearrange("g c d -> c g d")), writes=["poolw"], dma_key="c2")
        P.op("pool", lambda: G_.memset(onesb[:], 1.0 / 128), writes=["onesb"])
        P.op("pool", lambda: G_.memset(ones1[:], 1.0), writes=["ones1"])
        P.op("pool", lambda: G_.memset(ones2[:], 2.0), writes=["ones2"])
        P.op("pool", lambda: G_.memset(epsb[:], EPS), writes=["epsb"])
        P.op("dve", lambda: V_.tensor_copy(out=ident_b[:], in_=ident_f[:]), reads=["ident_f"], writes=["ident_b"])
        P.op("dve", lambda: V_.tensor_single_scalar(out=bmh[:], in_=vecs[:, 28:76], scalar=0.5, op=ALU.mult), reads=["vecs"], writes=["bmh"])
        P.op("dve", lambda: V_.tensor_single_scalar(out=psc[:], in_=vecs[:, 24:28], scalar=0.5, op=ALU.mult), reads=["vecs"], writes=["psc"])
        for (c0_, c1_, d0_) in ((0, 8, 0), (16, 20, 8)):
            P.op("dve", (lambda c0_=c0_, c1_=c1_, d0_=d0_: V_.tensor_single_scalar(
                out=gqt[:, d0_:d0_ + (c1_ - c0_)], in_=vecs[:, c0_:c1_], scalar=S_LO, op=ALU.mult)), reads=["vecs"], writes=[("gqt", d0_)])
            P.op("dve", (lambda c0_=c0_, c1_=c1_, d0_=d0_: V_.scalar_tensor_tensor(
                out=gqs[:, d0_:d0_ + (c1_ - c0_)], in0=vecs[:, c0_:c1_], scalar=S_HI, in1=gqt[:, d0_:d0_ + (c1_ - c0_)],
                op0=ALU.mult, op1=ALU.add)), reads=["vecs", ("gqt", d0_)], writes=["gqs"])

        single = Rot([0, 1, 2, 3])
        pairs = Rot([4, 6])

        def pipeline2_pieces(units):
            st = {}
            pcs = []
            n = len(units)

            def mk1(i):
                return lambda: st.__setitem__(i, units[i][0]())

            def mk2(i):
                def f():
                    r = units[i][1](st[i])
                    if r is not None:
                        st[i] = r
                return f

            def mk3(i):
                return lambda: units[i][2](st[i])
            three = n > 0 and len(units[0]) == 3
            for i in range(n):
                pcs.append(mk1(i))
                if i >= 1:
                    pcs.append(mk2(i - 1))
                if three and i >= 2:
                    pcs.append(mk3(i - 2))
            if n:
                pcs.append(mk2(n - 1))
                if three:
                    if n >= 2:
                        pcs.append(mk3(n - 2))
                    pcs.append(mk3(n - 1))
            return pcs

        def interleave(a, b):
            a = list(a)
            b = list(b)
            i = j = 0
            while i < len(a) or j < len(b):
                if i < len(a):
                    a[i]()
                    i += 1
                if j < len(b):
                    b[j]()
                    j += 1

        def pipeline2(units):
            interleave(pipeline2_pieces(units), [])

        def phase0_pieces(src, row0, ntiles, gsrc, dstT, dkey, tile0=0, xq="sp", big=False):
            fence3()
            ar = Arena(arena3)
            if big:
                fence()
                ar_a = Arena()
                xt = [ar.f32(2048) for _ in range(3)] + [ar_a.f32(2048)]
                xk = [("C", "xt", 0), ("C", "xt", 1), ("C", "xt", 2), ("A", "xt", 3)]
                hb = [ar.bf16(2048) for _ in range(2)]
                gbc = ar_a.f32(2048)
                gk = ("A", "gbc")
                junk = ar_a.bf16(2048)
            else:
                xt = [ar.f32(2048) for _ in range(2)]
                xk = [("C", "xt", 0), ("C", "xt", 1)]
                hb = [ar.bf16(2048) for _ in range(2)]
                gbc = ar.f32(2048)
                gk = ("C", "gbc")
            nx = len(xt)

            def pg():
                P.op("sp", lambda: SP.dma_start(out=gbc, in_=gsrc[:, :]), writes=[gk], dma_key="g")
            units = []
            XQ = {"sp": SP, "pool": G_, "act": A_}[xq]
            xkey = {"sp": "xt", "pool": "xtp", "act": "xta"}[xq]

            def xdma(i):
                s = i % nx
                P.op(xq, lambda: XQ.dma_start(out=xt[s], in_=src[row0 + i * 128: row0 + (i + 1) * 128, :]),
                     writes=[xk[s]], dma_key=(xkey, s))
            for i in range(tile0, ntiles):
                def s1(i=i):
                    s = i % nx
                    hs = i % 2
                    if xq == "act":
                        if i == tile0:
                            xdma(i)
                        if i + 1 < ntiles:
                            xdma(i + 1)
                    else:
                        xdma(i)
                    if big:
                        P.op("act", lambda: A_.activation(out=junk, in_=xt[s], func=AF.Square, scale=SQ_SCALE,
                                                          accum_out=stat[:, i:i + 1]),
                             reads=[xk[s], "arena"], writes=[("stat", i)])
                    else:
                        P.op("act", lambda: A_.activation(out=hb[hs], in_=xt[s], func=AF.Square, scale=SQ_SCALE,
                                                          accum_out=stat[:, i:i + 1]),
                             reads=[xk[s]], writes=[("C", "hb", hs), ("stat", i)])
                    P.op("act", lambda: A_.activation(out=stat[:, 16 + i:17 + i], in_=stat[:, i:i + 1], func=AF.Ln, bias=epsb[:, 0:1], scale=LN_SCALE),
                         reads=[("stat", i), "epsb"], writes=[("stat", 16 + i)])
                    P.op("act", lambda: A_.activation(out=stat[:, 32 + i:33 + i], in_=stat[:, 16 + i:17 + i], func=AF.Exp, scale=-0.5),
                         reads=[("stat", 16 + i)], writes=[("stat", 32 + i)])
                    if not big:
                        P.op("dve", lambda: V_.scalar_tensor_tensor(out=hb[hs], in0=xt[s], scalar=stat[:, 32 + i:33 + i], in1=gbc,
                                                                    op0=ALU.mult, op1=ALU.mult),
                             reads=[xk[s], ("stat", 32 + i), gk], writes=[("C", "hb", hs)])
                    return (s, hs)

                def s2(st, i=i):
                    s, hs = st
                    if big:
                        P.op("dve", lambda: V_.scalar_tensor_tensor(out=hb[hs], in0=xt[s], scalar=stat[:, 32 + i:33 + i], in1=gbc,
                                                                    op0=ALU.mult, op1=ALU.mult),
                             reads=[xk[s], ("stat", 32 + i), gk], writes=[("C", "hb", hs)])
                    s = hs
                    b0 = pairs.next()
                    pT = pair(b0).bitcast(BF16)
                    for kc in range(NKC):
                        P.op("pe", (lambda kc=kc: PE.transpose(out=pT[:, kc * 128:(kc + 1) * 128], in_=hb[s][:, kc * 128:(kc + 1) * 128],
                                                                identity=ident_b[:])),
                             reads=[("C", "hb", s), "ident_b"], writes=[("ps", b0), ("ps", b0 + 1)])
                    return (b0, pT)

                def s3(st, i=i):
                    b0, pT = st
                    if big and i % 3 != 0:
                        P.op("dve", lambda: V_.tensor_copy(out=dstT[:, :, i * 128:(i + 1) * 128],
                                                           in_=pT.rearrange("p (k t) -> p k t", k=NKC)),
                             reads=[("ps", b0), ("ps", b0 + 1)], writes=[(dkey, i)])
                    else:
                        P.op("act", lambda: A_.activation(out=dstT[:, :, i * 128:(i + 1) * 128],
                                                          in_=pT.rearrange("p (k t) -> p k t", k=NKC), func=AF.Copy),
                             reads=[("ps", b0), ("ps", b0 + 1)], writes=[(dkey, i)])
                if big:
                    units.append((s1, s2, s3))
                else:
                    units.append((s1, (lambda st, s2=s2, s3=s3: s3(s2(st)))))
            pre = [pg]
            if tile0 > 0:
                def cp():
                    n0 = (ntiles - tile0) * 128
                    for hh in range(2):
                        P.op("dve", (lambda hh=hh: V_.tensor_copy(out=dstT[:, hh * 8:(hh + 1) * 8, 0:tile0 * 128],
                                                                  in_=dstT[:, hh * 8:(hh + 1) * 8, n0:n0 + tile0 * 128])),
                             reads=[(dkey, t_) for t_ in range(ntiles - tile0, ntiles)], writes=[(dkey, t_) for t_ in range(tile0)])
                pre.append(cp)
            return pre + pipeline2_pieces(units)

        def norm_units(wt, wkey, col0, src, skeys_fn, tok_segs, gain_ap, dst_fn, dkeys_fn, sq, lnb, units, lnp="A", raw=None):
            for si, (t0, n) in enumerate(tok_segs):
                def s1(si=si, t0=t0, n=n):
                    a = single.next()
                    j = wstate.setdefault("nu", 0) % len(raw)
                    wstate["nu"] += 1
                    for kc in range(NKC):
                        P.op("pe", (lambda kc=kc: PE.matmul(bank(a)[:, 0:n], lhsT=wt[:, kc, col0:col0 + 128], rhs=src[:, kc, t0:t0 + n],
                                                            start=(kc == 0), stop=(kc == NKC - 1))),
                             reads=[wkey] + skeys_fn(t0, n), writes=[("ps", a)])
                    P.op("act", lambda: A_.activation(out=sq[j][:, 0:n], in_=bank(a)[:, 0:n], func=AF.Square),
                         reads=[("ps", a)], writes=[(lnp, "sq", j)])
                    P.op("act", lambda: A_.activation(out=raw[j][:, 0:n], in_=bank(a)[:, 0:n], func=AF.Copy),
                         reads=[("ps", a)], writes=[(lnp, "raw", j)])
                    return (a, j)

                def s2(st, si=si, t0=t0, n=n):
                    a, j = st
                    b = single.next()
                    P.op("pe", lambda: PE.matmul(bank(b)[:, 0:n], lhsT=onesb[:], rhs=sq[j][:, 0:n], start=True, stop=True),
                         reads=[(lnp, "sq", j), "onesb"], writes=[("ps", b)])
                    P.op("act", lambda: A_.activation(out=lnb[j][:, 0:n], in_=bank(b)[:, 0:n], func=AF.Ln, bias=epsb[:, 0:1], scale=1.0),
                         reads=[("ps", b), "epsb"], writes=[(lnp, "ln", j)])
                    P.op("act", lambda: A_.activation(out=lnb[j][:, 0:n], in_=lnb[j][:, 0:n], func=AF.Exp, scale=-0.5),
                         reads=[(lnp, "ln", j)], writes=[(lnp, "ln", j)])
                    return st

                def s3(st, si=si, t0=t0, n=n):
                    a, j = st
                    P.op("dve", lambda: V_.scalar_tensor_tensor(out=dst_fn(si, n), in0=raw[j][:, 0:n], scalar=gain_ap, in1=lnb[j][:, 0:n],
                                                                op0=ALU.mult, op1=ALU.mult),
                         reads=[(lnp, "raw", j), (lnp, "ln", j), "vecs", "gqs"], writes=dkeys_fn(si))
                units.append((s1, s2, s3))

        def hkeys(t0, n):
            return [("hT", t) for t in range(t0 // 128, (t0 + n + 127) // 128)]

        def gate_u(wt, wkey, col0, seg, th, u_dst, ukeys):
            a = single.next()
            t0 = HALO + seg * 512
            j = wstate.setdefault("gu", 0) % 2
            wstate["gu"] += 1
            for kc in range(NKC):
                P.op("pe", (lambda kc=kc: PE.matmul(bank(a), lhsT=wt[:, kc, col0:col0 + 128], rhs=hT[:, kc, t0:t0 + 512],
                                                    start=(kc == 0), stop=(kc == NKC - 1))),
                     reads=[wkey] + hkeys(t0, 512), writes=[("ps", a)])
            P.op("act", lambda: A_.activation(out=th[j], in_=bank(a), func=AF.Tanh, scale=0.5),
                 reads=[("ps", a)], writes=[("A", "th", j)])
            P.op("dve", lambda: V_.scalar_tensor_tensor(out=u_dst, in0=th[j], scalar=1.0, in1=bank(a), op0=ALU.add, op1=ALU.mult),
                 reads=[("A", "th", j), ("ps", a)], writes=ukeys)

        interleave(phase0_pieces(xs, 0, SL // 128, gain_d, hT, "hT", big=True), [])
        memT2 = o_flat[:, 12 * T:16 * T].rearrange("p (k t) -> p k t", k=NKC)
        MK = [("o", b_, s_) for b_ in range(12, 16) for s_ in range(2)]

        for p in range(NPASS):

            fence()
            fence2()
            fence3()
            ar = Arena()
            ar2 = Arena(arena2)
            sets = [dict(qn=ar.bf16(T), kn=ar.bf16(SL), Vt=ar.bf16(SL), u=ar.f32(T), i=i) for i in range(2)]
            rec = [ar.f32(128) for _ in range(2)]
            tq = [ar.f32(128) for _ in range(2)]
            th = [ar.f32(512) for _ in range(2)]
            lnb = [ar2.f32(512) for _ in range(3)]
            raw = [ar2.f32(512) for _ in range(3)]
            sq = [ar2.bf16(512) for _ in range(3)]
            tmpS = [ar2.f32(640) for _ in range(3)]
            PT = [ar2.bf16(640) for _ in range(3)]

            def proj_pieces(h, S, wt, wkey):
                si_ = S["i"]
                qn, kn, Vt, u = S["qn"], S["kn"], S["Vt"], S["u"]
                bs = h % 2
                koff = 0 if p == 0 else 1

                def p0():
                    P.op("sp", lambda: SP.dma_start(out=Bh[bs][:], in_=relb_d[h, :, :]), writes=[("Bh", bs)], dma_key=("Bh", bs))
                    P.op("pool", lambda: G_.memset(Bh[bs][64:128, 0:64], NEG), reads=[("Bh", bs)], writes=[("Bh", bs)])
                    P.op("pool", lambda: G_.memset(Bh[bs][0:64, 576:640], NEG), reads=[("Bh", bs)], writes=[("Bh", bs)])
                    P.op("act", lambda: A_.activation(out=Bh[bs][:], in_=Bh[bs][:], func=AF.Exp), reads=[("Bh", bs)], writes=[("Bh", bs)])
                    if p > 0:
                        P.op("sp", lambda: SP.dma_start(out=kn[:, 0:HALO], in_=kvs_k[h, :, :]), reads=[("kvs", h)],
                             writes=[("A", "kn", si_, 0)], dma_key=("kvl", si_))
                        P.op("sp", lambda: SP.dma_start(out=Vt[:, 0:HALO], in_=kvs_v[h, :, :]), reads=[("kvs", h)],
                             writes=[("A", "V", si_, 0)], dma_key=("kvl", si_))
                units = []
                ksegs = [(0, 512), (512, 512), (1024, 512)]
                norm_units(wt, wkey, 128, hT, hkeys, ksegs if p == 0 else ksegs[1:], vecs[:, 8 + h:9 + h],
                           (lambda si, n: kn[:, (si + koff) * 512:(si + koff) * 512 + n]),
                           (lambda si: [("A", "kn", si_, si + koff)]), sq, lnb, units, lnp="B", raw=raw)
                norm_units(wt, wkey, 0, hT, hkeys, [(HALO, 512), (HALO + 512, 512)], gqs[:, h:h + 1],
                           (lambda si, n: qn[:, si * 512:si * 512 + n]), (lambda si: [("A", "qn", si_, si)]), sq, lnb, units, lnp="B", raw=raw)
                pcs = [p0] + pipeline2_pieces(units)

                def vgroup(tg):
                    a = single.next()
                    for t4 in range(4):
                        tt = tg * 4 + t4
                        for kc in range(NKC):
                            P.op("pe", (lambda kc=kc, t4=t4, tt=tt: PE.matmul(
                                bank(a)[:, t4 * 128:(t4 + 1) * 128], lhsT=hT[:, kc, tt * 128:(tt + 1) * 128], rhs=wt[:, kc, 256:384],
                                start=(kc == 0), stop=(kc == NKC - 1))),
                                reads=[wkey, ("hT", tt)], writes=[("ps", a)])
                    P.op("act", lambda: A_.activation(out=Vt[:, tg * 512:(tg + 1) * 512], in_=bank(a), func=AF.Copy),
                         reads=[("ps", a)], writes=[("A", "V", si_, tg)])
                for tg in range(koff, 3):
                    pcs.append(lambda tg=tg: vgroup(tg))
                for seg in range(2):
                    pcs.append(lambda seg=seg: gate_u(wt, wkey, 384, seg, th, u[:, seg * 512:(seg + 1) * 512], [("A", "u", si_, seg)]))
                if p + 1 < NPASS:
                    def save():
                        P.op("sp", lambda: SP.dma_start(out=kvs_k[h, :, :], in_=kn[:, T:T + HALO]), reads=[("A", "kn", si_, 2)],
                             writes=[("kvs", h)], dma_key="kvs_st")
                        P.op("sp", lambda: SP.dma_start(out=kvs_v[h, :, :], in_=Vt[:, T:T + HALO]), reads=[("A", "V", si_, 2)],
                             writes=[("kvs", h)], dma_key="kvs_st")
                    pcs.append(save)
                return pcs

            def attn_pieces(h, S):
                si_ = S["i"]
                qn, kn, Vt, u = S["qn"], S["kn"], S["Vt"], S["u"]
                bs = h % 2
                units = []
                for r in range(8):
                    def s1(r=r):
                        b0 = pairs.next()
                        pa = pair(b0)
                        i = r % 3
                        for m in range(5):
                            kt = r + 4 - m
                            P.op("pe", (lambda m=m, kt=kt: PE.matmul(pa[:, m * 128:(m + 1) * 128], lhsT=kn[:, kt * 128:(kt + 1) * 128],
                                                                      rhs=qn[:, r * 128:(r + 1) * 128], start=True, stop=True)),
                                 reads=[("A", "kn", si_, kt // 4), ("A", "qn", si_, r // 4)], writes=[("ps", b0 if m < 4 else b0 + 1)])
                        if p == 0 and r < 4:
                            c = (r + 1) * 128
                            P.op("act", lambda: A_.activation(out=tmpS[i][:, 0:c], in_=pa[:, 0:c], func=AF.Exp),
                                 reads=[("ps", b0), ("ps", b0 + 1)], writes=[("B", "tmpS", i)])
                            P.op("act", lambda: A_.activation(out=tmpS[i][:, c:640], in_=pa[:, c:640], func=AF.Exp, bias=kmask[:, 0:1], scale=1.0),
                                 reads=[("ps", b0), ("ps", b0 + 1), "kmask"], writes=[("B", "tmpS", i)])
                        else:
                            P.op("act", lambda: A_.activation(out=tmpS[i], in_=pa[:, 0:640], func=AF.Exp),
                                 reads=[("ps", b0), ("ps", b0 + 1)], writes=[("B", "tmpS", i)])
                        return (b0, i)

                    def s2(st, r=r):
                        b0, i = st
                        P.op("pool", lambda: G_.tensor_tensor(out=PT[i], in0=tmpS[i], in1=Bh[bs][:], op=ALU.mult),
                             reads=[("B", "tmpS", i), ("Bh", bs)], writes=[("B", "PT", i)])

                    def s3(st, r=r):
                        b0, i = st
                        ri = r % 2
                        c = single.next()
                        pc = bank(c)
                        for m in range(5):
                            kt = r + 4 - m
                            P.op("pe", (lambda m=m, kt=kt: PE.matmul(pc[:, 0:128], lhsT=Vt[:, kt * 128:(kt + 1) * 128],
                                                                      rhs=PT[i][:, m * 128:(m + 1) * 128], start=(m == 0), stop=(m == 4))),
                                 reads=[("A", "V", si_, kt // 4), ("B", "PT", i)], writes=[("ps", c)])
                        for m in range(5):
                            P.op("pe", (lambda m=m: PE.matmul(pc[:, 128:256], lhsT=ones1[:], rhs=PT[i][:, m * 128:(m + 1) * 128],
                                                               start=(m == 0), stop=(m == 4))),
                                 reads=["ones1", ("B", "PT", i)], writes=[("ps", c)])
                        P.op("dve", lambda: V_.reciprocal(out=rec[ri], in_=pc[:, 128:256]), reads=[("ps", c)], writes=[("A", "rec", ri)])
                        P.op("dve", lambda: V_.tensor_tensor(out=tq[ri], in0=pc[:, 0:128], in1=rec[ri], op=ALU.mult),
                             reads=[("ps", c), ("A", "rec", ri)], writes=[("A", "tq", ri)])
                        P.op("dve", lambda: V_.scalar_tensor_tensor(out=o_sb[:, h, r * 128:(r + 1) * 128], in0=tq[ri], scalar=0.5,
                                                                    in1=u[:, r * 128:(r + 1) * 128], op0=ALU.mult, op1=ALU.mult),
                             reads=[("A", "tq", ri), ("A", "u", si_, r // 4)], writes=[("o", h, r // 4)])
                    units.append((s1, s2, s3))
                return pipeline2_pieces(units)

            prev_attn = []
            for h in range(8):
                wt, wkey = wnext("head%d" % h)
                interleave(prev_attn, proj_pieces(h, sets[h % 2], wt, wkey))
                prev_attn = attn_pieces(h, sets[h % 2])
            interleave(prev_attn, [])

            fence()
            ar = Arena()
            va = ar.f32(1152)
            X = ar.f32(1152)
            Y = ar.f32(1152)
            pooled = [ar.bf16(T) for _ in range(4)]
            th = [ar.f32(512) for _ in range(2)]
            ub = [ar.f32(512) for _ in range(2)]
            mem_pcs = []
            if p == 0:
                fence2()
                arm = Arena(arena2)
                xt_m = [arm.f32(2048) for _ in range(2)]
                gbc_m = arm.f32(2048)
                hb_m = arm.bf16(2048)
                P.op("sp", lambda: SP.dma_start(out=gbc_m, in_=mgain_d[:, :]), writes=[("B", "gbcm")], dma_key="g")
                for i_ in range(2):
                    P.op("sp", (lambda i_=i_: SP.dma_start(out=xt_m[i_], in_=mem[i_ * 128:(i_ + 1) * 128, :])),
                         writes=[("B", "xtm", i_)], dma_key=("xtm", i_))

                def mem_tile(i):
                    def f():
                        P.op("act", lambda: A_.activation(out=hb_m, in_=xt_m[i], func=AF.Square, scale=SQ_SCALE, accum_out=stat[:, 48 + i:49 + i]),
                             reads=[("B", "xtm", i)], writes=[("B", "hbm"), ("stat", 48 + i)])
                        P.op("act", lambda: A_.activation(out=stat[:, 52 + i:53 + i], in_=stat[:, 48 + i:49 + i], func=AF.Ln, bias=epsb[:, 0:1], scale=LN_SCALE),
                             reads=[("stat", 48 + i), "epsb"], writes=[("stat", 52 + i)])
                        P.op("act", lambda: A_.activation(out=stat[:, 56 + i:57 + i], in_=stat[:, 52 + i:53 + i], func=AF.Exp, scale=-0.5),
                             reads=[("stat", 52 + i)], writes=[("stat", 56 + i)])
                        P.op("dve", lambda: V_.scalar_tensor_tensor(out=hb_m, in0=xt_m[i], scalar=stat[:, 56 + i:57 + i], in1=gbc_m,
                                                                    op0=ALU.mult, op1=ALU.mult),
                             reads=[("B", "xtm", i), ("stat", 56 + i), ("B", "gbcm")], writes=[("B", "hbm")])
                    return f

                def mem_tr(i):
                    def f():
                        b0 = pairs.next()
                        pT = pair(b0).bitcast(BF16)
                        for kc in range(NKC):
                            P.op("pe", (lambda kc=kc: PE.transpose(out=pT[:, kc * 128:(kc + 1) * 128], in_=hb_m[:, kc * 128:(kc + 1) * 128],
                                                                    identity=ident_b[:])),
                                 reads=[("B", "hbm"), "ident_b"], writes=[("ps", b0), ("ps", b0 + 1)])
                        P.op("act", lambda: A_.activation(out=memT2[:, :, i * 128:(i + 1) * 128],
                                                          in_=pT.rearrange("p (k t) -> p k t", k=NKC), func=AF.Copy),
                             reads=[("ps", b0), ("ps", b0 + 1)], writes=MK)
                    return f
                t0_, r0_, t1_, r1_ = mem_tile(0), mem_tr(0), mem_tile(1), mem_tr(1)
                mem_pcs = []
                mem_late = [(lambda: None), t0_, (lambda: (r0_(), t1_())), r1_]
            wt, wkey = wnext("vb")
            for g in range(4):
                if mem_pcs:
                    mem_pcs.pop(0)()
                for (t0, n, c0) in ((384, 128, 0), (512, 512, 128), (1024, 512, 640)):
                    a = single.next()
                    for kc in range(NKC):
                        P.op("pe", (lambda kc=kc, a=a, t0=t0, n=n, g=g, wt=wt: PE.matmul(
                            bank(a)[:, 0:n], lhsT=wt[:, kc, g * 128:(g + 1) * 128], rhs=hT[:, kc, t0:t0 + n],
                            start=(kc == 0), stop=(kc == NKC - 1))),
                            reads=[wkey] + hkeys(t0, n), writes=[("ps", a)])
                    P.op("act", (lambda a=a, n=n, c0=c0: A_.activation(out=va[:, c0:c0 + n], in_=bank(a)[:, 0:n], func=AF.Copy)),
                         reads=[("ps", a)], writes=[("A", "va")])
                srcb, k, sh = va, "va", 1
                bufs = [(X, "X"), (Y, "Y")]
                for lvl in range(g + 1):
                    dstb, dk = bufs[lvl % 2]
                    lo = 2 * sh - 1
                    P.op("pool", (lambda dstb=dstb, srcb=srcb, lo=lo, sh=sh: G_.tensor_tensor(
                        out=dstb[:, lo:1152], in0=srcb[:, lo:1152], in1=srcb[:, lo - sh:1152 - sh], op=ALU.add)),
                        reads=[("A", k)], writes=[("A", dk)])
                    srcb, k, sh = dstb, dk, sh * 2
                if p == 0:
                    P.op("pool", (lambda srcb=srcb, g=g: G_.tensor_tensor(out=srcb[:, 128:144], in0=srcb[:, 128:144],
                                                                          in1=pcorr[:, g * 16:(g + 1) * 16], op=ALU.mult)),
                         reads=[("A", k), "pcorr"], writes=[("A", k)])
                P.op("dve", (lambda srcb=srcb, g=g: V_.scalar_tensor_tensor(out=pooled[g], in0=srcb[:, 128:1152], scalar=1.0 / POOL_W[g],
                                                                            in1=va[:, 128:1152], op0=ALU.mult, op1=ALU.subtract)),
                     reads=[("A", k), ("A", "va")], writes=[("A", "pooled", g)])
            wt, wkey = wnext("gb")
            for g in range(4):
                if p == 0:
                    mem_late[g]()
                for seg in range(2):
                    j = wstate.setdefault("gu", 0) % 2
                    gate_u(wt, wkey, g * 128, seg, th, ub[j], [("A", "ub", j)])
                    a = single.next()
                    P.op("pe", (lambda a=a, g=g, seg=seg: PE.matmul(bank(a), lhsT=poolw[:, g, :], rhs=pooled[g][:, seg * 512:(seg + 1) * 512],
                                                                    start=True, stop=True)),
                         reads=["poolw", ("A", "pooled", g)], writes=[("ps", a)])
                    P.op("dve", (lambda a=a, g=g, seg=seg, j=j: V_.scalar_tensor_tensor(
                        out=o_sb[:, 8 + g, seg * 512:(seg + 1) * 512], in0=bank(a), scalar=psc[:, g:g + 1], in1=ub[j],
                        op0=ALU.mult, op1=ALU.mult)),
                        reads=[("ps", a), "psc", ("A", "ub", j)], writes=[("o", 8 + g, seg)])

            fence()
            fence2()
            ar = Arena()
            ar2 = Arena(arena2)
            qn4 = [ar.bf16(T) for _ in range(4)]
            th = [ar.f32(512) for _ in range(2)]
            ub = [ar.f32(512) for _ in range(2)]
            PT2 = [ar.bf16(1024) for _ in range(2)]
            rec2 = [ar.f32(512) for _ in range(2)]
            sq = [ar2.bf16(512) for _ in range(3)]
            lnb = [ar2.f32(512) for _ in range(3)]
            raw = [ar2.f32(512) for _ in range(3)]
            wt, wkey = wnext("qm")
            units = []
            for h in range(4):
                norm_units(wt, wkey, h * 128, hT, hkeys, [(HALO, 512), (HALO + 512, 512)], gqs[:, 8 + h:9 + h],
                           (lambda si, n, h=h: qn4[h][:, si * 512:si * 512 + n]), (lambda si, h=h: [("A", "qn4", h, si)]), sq, lnb, units, lnp="B", raw=raw)
            pipeline2(units)
            if p == 0:
                wt, wkey = wnext("memk")
                units = []
                for h in range(4):
                    norm_units(wt, wkey, h * 128, memT2, (lambda t0, n: MK), [(0, 256)], vecs[:, 20 + h:21 + h],
                               (lambda si, n, h=h: mk_n[:, h, :]), (lambda si, h=h: [("mk", h)]), sq, lnb, units, lnp="B", raw=raw)
                pipeline2(units)
                wt, wkey = wnext("memv")
                for tt in range(2):
                    a = single.next()
                    for kc in range(NKC):
                        P.op("pe", (lambda kc=kc, a=a, tt=tt, wt=wt: PE.matmul(bank(a), lhsT=memT2[:, kc, tt * 128:(tt + 1) * 128], rhs=wt[:, kc, :],
                                                                               start=(kc == 0), stop=(kc == NKC - 1))),
                             reads=[wkey] + MK, writes=[("ps", a)])
                    P.op("act", (lambda a=a, tt=tt: A_.activation(out=mv[:, tt, :], in_=bank(a), func=AF.Copy)),
                         reads=[("ps", a)], writes=[("mv", tt)])
            wt, wkey = wnext("gm")
            units = []
            for h in range(4):
                for seg in range(2):
                    def s1(h=h, seg=seg, wt=wt, wkey=wkey):
                        j = wstate.setdefault("gu", 0) % 2
                        gate_u(wt, wkey, h * 128, seg, th, ub[j], [("A", "ub", j)])
                        b0 = pairs.next()
                        pa = pair(b0)
                        for jj in range(2):
                            P.op("pe", (lambda jj=jj: PE.matmul(pa[:, jj * 512:(jj + 1) * 512], lhsT=mk_n[:, h, jj * 128:(jj + 1) * 128],
                                                                rhs=qn4[h][:, seg * 512:(seg + 1) * 512], start=True, stop=True)),
                                 reads=[("mk", h), ("A", "qn4", h, seg)], writes=[("ps", b0 + jj)])
                        P.op("act", lambda: A_.activation(out=PT2[j], in_=pa, func=AF.Exp),
                             reads=[("ps", b0), ("ps", b0 + 1)], writes=[("A", "PT2", j)])
                        return j

                    def s2(j, h=h, seg=seg):
                        c = single.next()
                        for jj in range(2):
                            P.op("pe", (lambda jj=jj: PE.matmul(bank(c), lhsT=mv[:, jj, h * 128:(h + 1) * 128],
                                                                rhs=PT2[j][:, jj * 512:(jj + 1) * 512], start=(jj == 0), stop=(jj == 1))),
                                 reads=[("mv", jj), ("A", "PT2", j)], writes=[("ps", c)])
                        d = single.next()
                        for jj in range(2):
                            P.op("pe", (lambda jj=jj: PE.matmul(bank(d), lhsT=ones2[:], rhs=PT2[j][:, jj * 512:(jj + 1) * 512],
                                                                start=(jj == 0), stop=(jj == 1))),
                                 reads=["ones2", ("A", "PT2", j)], writes=[("ps", d)])
                        P.op("dve", lambda: V_.reciprocal(out=rec2[j], in_=bank(d)), reads=[("ps", d)], writes=[("A", "rec2", j)])
                        P.op("dve", lambda: V_.tensor_tensor(out=rec2[j], in0=bank(c), in1=rec2[j], op=ALU.mult),
                             reads=[("ps", c), ("A", "rec2", j)], writes=[("A", "rec2", j)])
                        P.op("pool", lambda: G_.tensor_tensor(out=o_sb[:, 12 + h, seg * 512:(seg + 1) * 512], in0=rec2[j], in1=ub[j], op=ALU.mult),
                             reads=[("A", "rec2", j), ("A", "ub", j)], writes=[("o", 12 + h, seg)])
                    units.append((s1, s2))
            pipeline2(units)

            fence()
            fence2()
            ar = Arena()
            thg = [[ar.f32(512) for _ in range(3)] for _ in range(2)]
            acc = [ar.f32(512) for _ in range(3)]
            ssum = ar.f32(512)
            xres = [ar.f32(512) for _ in range(3)]
            res = [ar.f32(512) for _ in range(2)]
            allb = Rot([4, 5, 6, 7, 0, 1, 2, 3])
            KB = [(0, 8), (8, 12), (12, 16)]
            for db in range(16):
                wt, wkey = wnext("mg%d" % db)
                for half in range(2):
                    j = (db * 2 + half) % 2
                    t0 = HALO + half * 512
                    zb = []
                    for br in range(3):
                        a = allb.next()
                        zb.append(a)
                        for kc in range(NKC):
                            P.op("pe", (lambda kc=kc, a=a, br=br, wt=wt, t0=t0: PE.matmul(
                                bank(a), lhsT=wt[:, kc, br * 128:(br + 1) * 128], rhs=hT[:, kc, t0:t0 + 512],
                                start=(kc == 0), stop=(kc == NKC - 1))),
                                reads=[wkey] + hkeys(t0, 512), writes=[("ps", a)])
                        P.op("act", (lambda a=a, br=br, j=j, db=db: A_.activation(out=thg[j][br], in_=bank(a), func=AF.Tanh,
                                                                                  bias=bmh[:, br * 16 + db:br * 16 + db + 1], scale=0.5)),
                             reads=[("ps", a), "bmh"], writes=[("A", "thg", j, br)])
                    for br in range(3):
                        a = allb.next()
                        k0, k1 = KB[br]
                        for kc in range(k0, k1):
                            P.op("pe", (lambda kc=kc, a=a, wt=wt, half=half, k0=k0, k1=k1: PE.matmul(
                                bank(a), lhsT=wt[:, kc, 384:512], rhs=o_sb[:, kc, half * 512:(half + 1) * 512],
                                start=(kc == k0), stop=(kc == k1 - 1))),
                                reads=[wkey, ("o", kc, half)], writes=[("ps", a)])
                        P.op("dve", (lambda a=a, br=br, j=j: V_.scalar_tensor_tensor(out=acc[br], in0=thg[j][br], scalar=1.0, in1=bank(a),
                                                                                     op0=ALU.add, op1=ALU.mult)),
                             reads=[("A", "thg", j, br), ("ps", a)], writes=[("A", "acc", br)])
                    P.op("pool", lambda: G_.tensor_tensor(out=ssum, in0=acc[0], in1=acc[1], op=ALU.add),
                         reads=[("A", "acc", 0), ("A", "acc", 1)], writes=[("A", "ssum")])
                    P.op("pool", (lambda db=db, half=half: G_.tensor_tensor(out=yp[:, db, half * 512:(half + 1) * 512], in0=ssum, in1=acc[2], op=ALU.add)),
                         reads=[("A", "ssum"), ("A", "acc", 2)], writes=[("yp", db)])

            def xload(un):
                c_, tt_ = un // 8, un % 8
                jx = un % 3
                r0 = HALO + p * T + tt_ * 128
                P.op("act", lambda: A_.dma_start(out=xres[jx], in_=xs[r0:r0 + 128, c_ * 512:(c_ + 1) * 512]),
                     writes=[("A", "xres", jx)], dma_key=("xres", jx))
            xload(0)
            p3 = []
            wcur = {}
            for c in range(4):
                for tt in range(8):
                    def unit(c=c, tt=tt):
                        if tt == 0:
                            wcur["w"] = wnext("wo%d" % c)
                        wt, wkey = wcur["w"]
                        un = c * 8 + tt
                        j = un % 2
                        jx = un % 3
                        a = single.next()
                        if un + 1 < 32:
                            xload(un + 1)
                        for kc in range(NKC):
                            P.op("pe", (lambda kc=kc: PE.matmul(bank(a), lhsT=yp[:, kc, tt * 128:(tt + 1) * 128], rhs=wt[:, kc, :],
                                                                start=(kc == 0), stop=(kc == NKC - 1))),
                                 reads=[wkey, ("yp", kc)], writes=[("ps", a)])
                        P.op("dve", lambda: V_.scalar_tensor_tensor(out=res[j], in0=bank(a), scalar=0.5, in1=xres[jx],
                                                                    op0=ALU.mult, op1=ALU.add),
                             reads=[("ps", a), ("A", "xres", jx)], writes=[("A", "res", j)])
                        orow = p * T + tt * 128
                        P.op("sp", lambda: SP.dma_start(out=out_d[orow:orow + 128, c * 512:(c + 1) * 512], in_=res[j]),
                             reads=[("A", "res", j)], dma_key=("out", j))
                    p3.append(unit)
            if p + 1 < NPASS:
                interleave(p3, phase0_pieces(xs, (p + 1) * T, SL // 128, gain_d, hT, "hT", tile0=HALO // 128, xq="act"))
            else:
                interleave(p3, [])

        P.emit()
    return nc


_CACHE = {}


def _program():
    if "nc" not in _CACHE:
        _CACHE["nc"] = build_program()
    return _CACHE["nc"]


def kernel(x, mem, norm_gain, mem_norm_gain, w_in, w_merge, b_merge, a_q_gain, a_k_gain,
           a_rel_bias, pool_w, pool_scale, w_mem_kv, m_q_gain, m_k_gain,
           w_branch_a, w_branch_b, w_branch_m, w_out):
    f = lambda a: np.ascontiguousarray(np.asarray(a, dtype=np.float32))
    x = f(x); mem = f(mem)
    B, S, _ = x.shape
    half_len = S // 2
    gain_bc = f(np.broadcast_to(np.asarray(norm_gain)[0][None, :], (128, D)))
    mgain_bc = f(np.broadcast_to(np.asarray(mem_norm_gain)[0][None, :], (128, D)))
    vecs = f(np.concatenate([
        np.asarray(a_q_gain)[0].T, np.asarray(a_k_gain)[0].T, np.asarray(m_q_gain)[0].T, np.asarray(m_k_gain)[0].T,
        np.asarray(pool_scale)[0].reshape(4, 128).T, np.asarray(b_merge)[0].reshape(48, 128).T], axis=1))
    ii = np.arange(128)[:, None]
    jj = np.arange(640)[None, :]
    idx = np.clip(jj - ii, -256, 256) + 256
    relb = f(np.asarray(a_rel_bias)[0][:, idx])
    ident = np.eye(128, dtype=np.float32)
    wstream = build_wstream(dict(
        w_in=np.asarray(w_in)[0], w_merge=np.asarray(w_merge)[0], w_mem_kv=np.asarray(w_mem_kv)[0],
        w_a=np.asarray(w_branch_a)[0], w_b=np.asarray(w_branch_b)[0], w_m=np.asarray(w_branch_m)[0],
        w_out=np.asarray(w_out)[0]))
    shared = dict(gain_bc=gain_bc, mgain_bc=mgain_bc, wstream=wstream, vecs=vecs, relb=relb,
                  pool_w=f(np.asarray(pool_w)[0]), ident=ident)
    in_maps = []
    tt = np.arange(16)
    for c in range(N_CORES):
        b, hf = c // 2, c % 2
        xs = np.zeros((HALO + half_len, D), np.float32)
        if hf == 0:
            xs[HALO:] = x[b, 0:half_len]
        else:
            xs[:] = x[b, half_len - HALO:S]
        km = np.full((128, 1), NEG if hf == 0 else 0.0, np.float32)
        pc = np.ones((128, 64), np.float32)
        if hf == 0:
            for g, w in enumerate(POOL_W):
                pc[:, g * 16:(g + 1) * 16] = (w / np.minimum(tt + 1, w)).astype(np.float32)[None, :]
        m = dict(shared)
        m.update(xs=xs, mem=mem[b], kmask=km, pcorr=pc)
        in_maps.append(m)
    nc = _program()
    r = run_bass_kernel_spmd(nc, in_maps, core_ids=list(range(N_CORES)))
    out = np.empty((B, S, D), np.float32)
    for c in range(N_CORES):
        b, hf = c // 2, c % 2
        out[b, hf * half_len:(hf + 1) * half_len] = r.results[c]["out"]
    return out
```

```python
import contextlib
import numpy as np
import concourse.bass as bass
import concourse.mybir as mybir
from concourse.bass_utils import run_bass_kernel_spmd

F32 = mybir.dt.float32
BF16 = mybir.dt.bfloat16
AF = mybir.ActivationFunctionType
ALU = mybir.AluOpType

D = 2048
T = 1024
HALO = 512
SL = T + HALO
NPASS = 2
NKC = 16
NSLOT = 3
EPS = 1e-6
NEG = -30000.0
SCALE = float(128 ** -0.5)
S_HI = float(np.float32(SCALE))
S_LO = float(np.float32(SCALE - S_HI))
SQ_SCALE = 2.0 ** -5
LN_SCALE = 0.5
assert abs(SQ_SCALE * SQ_SCALE * LN_SCALE * D - 1.0) < 1e-15
POOL_W = (2, 4, 8, 16)
N_CORES = 8


class _Op:
    __slots__ = ("eng", "fn", "dma_key", "needs_inc", "count", "waits", "idx")


class Prog:
    def __init__(self, nc):
        self.nc = nc
        self.eng = dict(pe=nc.tensor, act=nc.scalar, dve=nc.vector, pool=nc.gpsimd, sp=nc.sync)
        self.ops = []
        self.res = {}
        self.dma_cnt = {}

    def op(self, eng, fn, reads=(), writes=(), dma_key=None):
        reads = list(reads)
        writes = list(writes)
        if any(isinstance(k, tuple) and k[0] == "A" for k in reads + writes):
            reads.append("arena")
        if any(isinstance(k, tuple) and k[0] == "B" for k in reads + writes):
            reads.append("arena2")
        if any(isinstance(k, tuple) and k[0] == "C" for k in reads + writes):
            reads.append("arena3")
        o = _Op()
        o.eng = eng
        o.fn = fn()
        o.dma_key = dma_key
        o.needs_inc = False
        o.count = None
        o.idx = len(self.ops)
        deps = []
        for k in reads:
            r = self.res.get(k)
            if r is not None and r[0] is not None:
                deps.append(r[0])
        for k in writes:
            r = self.res.get(k)
            if r is not None:
                if r[0] is not None:
                    deps.append(r[0])
                deps.extend(r[1])
        for k in reads:
            r = self.res.get(k)
            if r is None:
                r = self.res[k] = [None, []]
            r[1].append(o)
        for k in writes:
            self.res[k] = [o, []]
        waits = {}
        for d in deps:
            if d is o:
                continue
            if d.dma_key is not None:
                key = ("dma", d.dma_key)
                v = 16 * self.dma_cnt[d.dma_key]
                if waits.get(key, (0, None))[0] < v:
                    waits[key] = (v, None)
            else:
                if d.eng == eng and eng == "pe":
                    continue
                d.needs_inc = True
                key = ("eng", d.eng)
                cur = waits.get(key)
                if cur is None or cur[1].idx < d.idx:
                    waits[key] = (None, d)
        o.waits = waits
        if dma_key is not None:
            self.dma_cnt[dma_key] = self.dma_cnt.get(dma_key, 0) + 1
        self.ops.append(o)
        return o

    def emit(self):
        nc = self.nc
        sems = {}
        with contextlib.ExitStack() as es:
            for e in self.eng:
                sems[("eng", e)] = es.enter_context(nc.semaphore("s_" + e))
            for i, k in enumerate(self.dma_cnt):
                sems[("dma", k)] = es.enter_context(nc.semaphore("d_%d" % i))
            cnt = {e: 0 for e in self.eng}
            waited = {e: {} for e in self.eng}
            for o in self.ops:
                E = self.eng[o.eng]
                for key, (v, d) in o.waits.items():
                    if d is not None:
                        v = d.count
                    if waited[o.eng].get(key, 0) >= v:
                        continue
                    waited[o.eng][key] = v
                    E.wait_ge(sems[key], v)
                m_, a_, k_ = o.fn
                ins = m_(*a_, **k_)
                if o.dma_key is not None:
                    ins.then_inc(sems[("dma", o.dma_key)], 16)
                elif o.needs_inc:
                    cnt[o.eng] += 1
                    o.count = cnt[o.eng]
                    ins.then_inc(sems[("eng", o.eng)], 1)
            E = self.eng["sp"]
            for k, n in self.dma_cnt.items():
                E.wait_ge(sems[("dma", k)], 16 * n)
            for e in self.eng:
                if cnt[e] > 0:
                    E.wait_ge(sems[("eng", e)], cnt[e])


class _Rec:
    def __init__(self, eng):
        self._e = eng

    def __getattr__(self, name):
        m = getattr(self._e, name)
        return lambda *a, **k: (m, a, k)


class Rot:
    def __init__(self, items):
        self.items = list(items)
        self.i = 0

    def next(self):
        v = self.items[self.i % len(self.items)]
        self.i += 1
        return v


def chunk_specs():
    pro = [("memk", [("w_mem_kv", 0, 2048, 0, 512, 0, 0)]),
           ("memv", [("w_mem_kv", 0, 2048, 512, 512, 0, 0)])]
    per = []
    for h in range(8):
        per.append(("head%d" % h, [("w_in", 0, 2048, j * 1024 + h * 128, 128, 0, j * 128) for j in range(4)]))
    per.append(("vb", [("w_in", 0, 2048, 4096, 512, 0, 0)]))
    per.append(("gb", [("w_in", 0, 2048, 4608, 512, 0, 0)]))
    per.append(("qm", [("w_in", 0, 2048, 5120, 512, 0, 0)]))
    per.append(("gm", [("w_in", 0, 2048, 5632, 512, 0, 0)]))
    for db in range(16):
        pcs = [("w_merge", 0, 2048, br * 2048 + db * 128, 128, 0, br * 128) for br in range(3)]
        pcs.append(("w_a", 0, 1024, db * 128, 128, 0, 384))
        pcs.append(("w_b", 0, 512, db * 128, 128, 1024, 384))
        pcs.append(("w_m", 0, 512, db * 128, 128, 1536, 384))
        per.append(("mg%d" % db, pcs))
    for c in range(4):
        per.append(("wo%d" % c, [("w_out", 0, 2048, c * 512, 512, 0, 0)]))
    return pro, per


def build_wstream(ws):
    pro, per = chunk_specs()
    allc = pro + per
    out = np.empty((len(allc), 128, NKC * 512), np.float32)
    for n, (tag, pcs) in enumerate(allc):
        m = np.empty((2048, 512), np.float32)
        for (src, r0, nr, c0, ncol, dr, dc) in pcs:
            m[dr:dr + nr, dc:dc + ncol] = ws[src][r0:r0 + nr, c0:c0 + ncol]
        out[n] = m.reshape(NKC, 128, 512).transpose(1, 0, 2).reshape(128, NKC * 512)
    return out


def build_program():
    nc = bass.Bass("TRN2", target_bir_lowering=False)

    def din(name, shape):
        return nc.dram_tensor(name, list(shape), F32, kind="ExternalInput").ap()

    xs = din("xs", [HALO + NPASS * T, D])
    mem = din("mem", [256, D])
    gain_d = din("gain_bc", [128, D])
    mgain_d = din("mgain_bc", [128, D])
    _pro, _per = chunk_specs()
    wstream_d = din("wstream", [len(_pro) + len(_per), 128, NKC * 512])
    vecs_d = din("vecs", [128, 76])
    relb_d = din("relb", [8, 128, 640])
    poolw_d = din("pool_w", [4, 128, 128])
    ident_d = din("ident", [128, 128])
    kmask_d = din("kmask", [128, 1])
    pcorr_d = din("pcorr", [128, 64])
    out_d = nc.dram_tensor("out", [NPASS * T, D], F32, kind="ExternalOutput").ap()
    kvs_k = nc.dram_tensor("kvs_k", [8, 128, HALO], BF16, kind="Internal").ap()
    kvs_v = nc.dram_tensor("kvs_v", [8, 128, HALO], BF16, kind="Internal").ap()

    es = contextlib.ExitStack()
    with es:
        def sb(name, shape, dt):
            return es.enter_context(nc.sbuf_tensor("sb_" + name, list(shape), dt))

        hT = sb("hT", [128, NKC, SL], BF16)
        o_flat = sb("o_sb", [128, 16 * T], BF16)
        o_sb = o_flat[:, :].rearrange("p (k t) -> p k t", k=16)
        yp_flat = sb("yp", [128, 16 * T], BF16)
        yp = yp_flat[:, :].rearrange("p (k t) -> p k t", k=16)
        memT = yp_flat[:, 0:NKC * 256].rearrange("p (k t) -> p k t", k=NKC)
        wsl = [sb("wsl%d" % i, [128, NKC, 512], BF16) for i in range(NSLOT)]
        vecs = sb("vecs", [128, 76], F32)
        bmh = sb("bmh", [128, 48], F32)
        gqs = sb("gqs", [128, 12], F32)
        gqt = sb("gqt", [128, 12], F32)
        psc = sb("psc", [128, 4], F32)
        ident_f = sb("ident_f", [128, 128], F32)
        ident_b = sb("ident_b", [128, 128], BF16)
        onesb = sb("onesb", [128, 128], BF16)
        ones1 = sb("ones1", [128, 128], BF16)
        ones2 = sb("ones2", [128, 128], BF16)
        epsb = sb("epsb", [128, 1], F32)
        kmask = sb("kmask", [128, 1], F32)
        pcorr = sb("pcorr", [128, 64], F32)
        poolw = sb("poolw", [128, 4, 128], BF16)
        mk_n = sb("mk_n", [128, 4, 256], BF16)
        mv = sb("mv", [128, 2, 512], BF16)
        Bh = [sb("Bh%d" % i, [128, 640], F32) for i in range(2)]
        stat = sb("stat", [128, 64], F32)
        dummy = sb("dummy", [128, 1], F32)
        dummy2 = sb("dummy2", [128, 1], F32)
        dummy3 = sb("dummy3", [128, 1], F32)
        AW = 8192
        arena = sb("arena", [128, AW], F32)
        ps = es.enter_context(nc.psum_tensor("ps", [128, 4096], F32))

        P = Prog(nc)
        V_ = _Rec(nc.vector)
        A_ = _Rec(nc.scalar)
        G_ = _Rec(nc.gpsimd)
        PE = _Rec(nc.tensor)
        SP = _Rec(nc.sync)

        arena2 = yp_flat[:, :].bitcast(F32)

        class Arena:
            def __init__(self, base=None):
                self.off = 0
                self.base = arena if base is None else base

            def f32(self, n):
                a = self.base[:, self.off:self.off + n]
                self.off += n
                assert self.off <= AW, self.off
                return a

            def bf16(self, n):
                assert n % 2 == 0
                a = self.base[:, self.off:self.off + n // 2].bitcast(BF16)
                self.off += n // 2
                assert self.off <= AW, self.off
                return a

        def fence():
            P.op("pool", lambda: G_.memset(dummy[:], 0.0), writes=["arena"])

        arena3 = o_flat[:, :].bitcast(F32)

        def fence3():
            P.op("pool", lambda: G_.memset(dummy3[:], 0.0),
                 writes=["arena3"] + [("o", b_, h_) for b_ in range(16) for h_ in range(2)])

        def fence2():
            P.op("pool", lambda: G_.memset(dummy2[:], 0.0), writes=["arena2"] + [("yp", i) for i in range(16)])

        def bank(i):
            return ps[:, i * 512:(i + 1) * 512]

        def pair(i):
            return ps[:, i * 512:(i + 2) * 512]

        chunks = [(t, i) for i, (t, _) in enumerate(_pro)]
        for p_ in range(NPASS):
            chunks += [(t, len(_pro) + i) for i, (t, _) in enumerate(_per)]
        wstate = dict(cur=-1, issued=0)

        def issue_chunk(n):
            s = n % NSLOT
            idx = chunks[n][1]
            P.op("pool", lambda: G_.dma_start(out=wsl[s][:], in_=wstream_d[idx].rearrange("p (k c) -> p k c", k=NKC)),
                 reads=([("hT", 1)] if n < NSLOT else []), writes=[("w", s)], dma_key=("w", s))

        def wnext(tag):
            wstate["cur"] += 1
            n = wstate["cur"]
            assert chunks[n][0] == tag, (chunks[n][0], tag)
            while wstate["issued"] < min(len(chunks), n + NSLOT):
                issue_chunk(wstate["issued"])
                wstate["issued"] += 1
            s = n % NSLOT
            return wsl[s], ("w", s)

        P.op("sp", lambda: SP.dma_start(out=vecs[:], in_=vecs_d[:, :]), writes=["vecs"], dma_key="c")
        P.op("sp", lambda: SP.dma_start(out=ident_f[:], in_=ident_d[:, :]), writes=["ident_f"], dma_key="c")
        P.op("sp", lambda: SP.dma_start(out=kmask[:], in_=kmask_d[:, :]), writes=["kmask"], dma_key="c")
        P.op("sp", lambda: SP.dma_start(out=pcorr[:], in_=pcorr_d[:, :]), writes=["pcorr"], dma_key="c")
        P.op("pool", lambda: G_.dma_start(out=poolw[:], in_=poolw_d.rearrange("g c d -> c g d")), writes=["poolw"], dma_key="c2")
        P.op("pool", lambda: G_.memset(onesb[:], 1.0 / 128), writes=["onesb"])
        P.op("pool", lambda: G_.memset(ones1[:], 1.0), writes=["ones1"])
        P.op("pool", lambda: G_.memset(ones2[:], 2.0), writes=["ones2"])
        P.op("pool", lambda: G_.memset(epsb[:], EPS), writes=["epsb"])
        P.op("dve", lambda: V_.tensor_copy(out=ident_b[:], in_=ident_f[:]), reads=["ident_f"], writes=["ident_b"])
        P.op("dve", lambda: V_.tensor_single_scalar(out=bmh[:], in_=vecs[:, 28:76], scalar=0.5, op=ALU.mult), reads=["vecs"], writes=["bmh"])
        P.op("dve", lambda: V_.tensor_single_scalar(out=psc[:], in_=vecs[:, 24:28], scalar=0.5, op=ALU.mult), reads=["vecs"], writes=["psc"])
        for (c0_, c1_, d0_) in ((0, 8, 0), (16, 20, 8)):
            P.op("dve", (lambda c0_=c0_, c1_=c1_, d0_=d0_: V_.tensor_single_scalar(
                out=gqt[:, d0_:d0_ + (c1_ - c0_)], in_=vecs[:, c0_:c1_], scalar=S_LO, op=ALU.mult)), reads=["vecs"], writes=[("gqt", d0_)])
            P.op("dve", (lambda c0_=c0_, c1_=c1_, d0_=d0_: V_.scalar_tensor_tensor(
                out=gqs[:, d0_:d0_ + (c1_ - c0_)], in0=vecs[:, c0_:c1_], scalar=S_HI, in1=gqt[:, d0_:d0_ + (c1_ - c0_)],
                op0=ALU.mult, op1=ALU.add)), reads=["vecs", ("gqt", d0_)], writes=["gqs"])

        single = Rot([0, 1, 2, 3])
        pairs = Rot([4, 6])

        def pipeline2_pieces(units):
            st = {}
            pcs = []
            n = len(units)

            def mk1(i):
                return lambda: st.__setitem__(i, units[i][0]())

            def mk2(i):
                def f():
                    r = units[i][1](st[i])
                    if r is not None:
                        st[i] = r
                return f

            def mk3(i):
                return lambda: units[i][2](st[i])
            three = n > 0 and len(units[0]) == 3
            for i in range(n):
                pcs.append(mk1(i))
                if i >= 1:
                    pcs.append(mk2(i - 1))
                if three and i >= 2:
                    pcs.append(mk3(i - 2))
            if n:
                pcs.append(mk2(n - 1))
                if three:
                    if n >= 2:
                        pcs.append(mk3(n - 2))
                    pcs.append(mk3(n - 1))
            return pcs

        def interleave(a, b):
            a = list(a)
            b = list(b)
            i = j = 0
            while i < len(a) or j < len(b):
                if i < len(a):
                    a[i]()
                    i += 1
                if j < len(b):
                    b[j]()
                    j += 1

        def pipeline2(units):
            interleave(pipeline2_pieces(units), [])

        def phase0_pieces(src, row0, ntiles, gsrc, dstT, dkey, tile0=0, xq="sp", big=False):
            fence3()
            ar = Arena(arena3)
            if big:
                fence()
                ar_a = Arena()
                xt = [ar.f32(2048) for _ in range(3)] + [ar_a.f32(2048)]
                xk = [("C", "xt", 0), ("C", "xt", 1), ("C", "xt", 2), ("A", "xt", 3)]
                hb = [ar.bf16(2048) for _ in range(2)]
                gbc = ar_a.f32(2048)
                gk = ("A", "gbc")
                junk = ar_a.bf16(2048)
            else:
                xt = [ar.f32(2048) for _ in range(2)]
                xk = [("C", "xt", 0), ("C", "xt", 1)]
                hb = [ar.bf16(2048) for _ in range(2)]
                gbc = ar.f32(2048)
                gk = ("C", "gbc")
            nx = len(xt)

            def pg():
                P.op("sp", lambda: SP.dma_start(out=gbc, in_=gsrc[:, :]), writes=[gk], dma_key="g")
            units = []
            XQ = {"sp": SP, "pool": G_, "act": A_}[xq]
            xkey = {"sp": "xt", "pool": "xtp", "act": "xta"}[xq]

            def xdma(i):
                s = i % nx
                P.op(xq, lambda: XQ.dma_start(out=xt[s], in_=src[row0 + i * 128: row0 + (i + 1) * 128, :]),
                     writes=[xk[s]], dma_key=(xkey, s))
            for i in range(tile0, ntiles):
                def s1(i=i):
                    s = i % nx
                    hs = i % 2
                    if xq == "act":
                        if i == tile0:
                            xdma(i)
                        if i + 1 < ntiles:
                            xdma(i + 1)
                    else:
                        xdma(i)
                    if big:
                        P.op("act", lambda: A_.activation(out=junk, in_=xt[s], func=AF.Square, scale=SQ_SCALE,
                                                          accum_out=stat[:, i:i + 1]),
                             reads=[xk[s], "arena"], writes=[("stat", i)])
                    else:
                        P.op("act", lambda: A_.activation(out=hb[hs], in_=xt[s], func=AF.Square, scale=SQ_SCALE,
                                                          accum_out=stat[:, i:i + 1]),
                             reads=[xk[s]], writes=[("C", "hb", hs), ("stat", i)])
                    P.op("act", lambda: A_.activation(out=stat[:, 16 + i:17 + i], in_=stat[:, i:i + 1], func=AF.Ln, bias=epsb[:, 0:1], scale=LN_SCALE),
                         reads=[("stat", i), "epsb"], writes=[("stat", 16 + i)])
                    P.op("act", lambda: A_.activation(out=stat[:, 32 + i:33 + i], in_=stat[:, 16 + i:17 + i], func=AF.Exp, scale=-0.5),
                         reads=[("stat", 16 + i)], writes=[("stat", 32 + i)])
                    if not big:
                        P.op("dve", lambda: V_.scalar_tensor_tensor(out=hb[hs], in0=xt[s], scalar=stat[:, 32 + i:33 + i], in1=gbc,
                                                                    op0=ALU.mult, op1=ALU.mult),
                             reads=[xk[s], ("stat", 32 + i), gk], writes=[("C", "hb", hs)])
                    return (s, hs)

                def s2(st, i=i):
                    s, hs = st
                    if big:
                        P.op("dve", lambda: V_.scalar_tensor_tensor(out=hb[hs], in0=xt[s], scalar=stat[:, 32 + i:33 + i], in1=gbc,
                                                                    op0=ALU.mult, op1=ALU.mult),
                             reads=[xk[s], ("stat", 32 + i), gk], writes=[("C", "hb", hs)])
                    s = hs
                    b0 = pairs.next()
                    pT = pair(b0).bitcast(BF16)
                    for kc in range(NKC):
                        P.op("pe", (lambda kc=kc: PE.transpose(out=pT[:, kc * 128:(kc + 1) * 128], in_=hb[s][:, kc * 128:(kc + 1) * 128],
                                                                identity=ident_b[:])),
                             reads=[("C", "hb", s), "ident_b"], writes=[("ps", b0), ("ps", b0 + 1)])
                    return (b0, pT)

                def s3(st, i=i):
                    b0, pT = st
                    if big and i % 3 != 0:
                        P.op("dve", lambda: V_.tensor_copy(out=dstT[:, :, i * 128:(i + 1) * 128],
                                                           in_=pT.rearrange("p (k t) -> p k t", k=NKC)),
                             reads=[("ps", b0), ("ps", b0 + 1)], writes=[(dkey, i)])
                    else:
                        P.op("act", lambda: A_.activation(out=dstT[:, :, i * 128:(i + 1) * 128],
                                                          in_=pT.rearrange("p (k t) -> p k t", k=NKC), func=AF.Copy),
                             reads=[("ps", b0), ("ps", b0 + 1)], writes=[(dkey, i)])
                if big:
                    units.append((s1, s2, s3))
                else:
                    units.append((s1, (lambda st, s2=s2, s3=s3: s3(s2(st)))))
            pre = [pg]
            if tile0 > 0:
                def cp():
                    n0 = (ntiles - tile0) * 128
                    for hh in range(2):
                        P.op("dve", (lambda hh=hh: V_.tensor_copy(out=dstT[:, hh * 8:(hh + 1) * 8, 0:tile0 * 128],
                                                                  in_=dstT[:, hh * 8:(hh + 1) * 8, n0:n0 + tile0 * 128])),
                             reads=[(dkey, t_) for t_ in range(ntiles - tile0, ntiles)], writes=[(dkey, t_) for t_ in range(tile0)])
                pre.append(cp)
            return pre + pipeline2_pieces(units)

        def norm_units(wt, wkey, col0, src, skeys_fn, tok_segs, gain_ap, dst_fn, dkeys_fn, sq, lnb, units, lnp="A", raw=None):
            for si, (t0, n) in enumerate(tok_segs):
                def s1(si=si, t0=t0, n=n):
                    a = single.next()
                    j = wstate.setdefault("nu", 0) % len(raw)
                    wstate["nu"] += 1
                    for kc in range(NKC):
                        P.op("pe", (lambda kc=kc: PE.matmul(bank(a)[:, 0:n], lhsT=wt[:, kc, col0:col0 + 128], rhs=src[:, kc, t0:t0 + n],
                                                            start=(kc == 0), stop=(kc == NKC - 1))),
                             reads=[wkey] + skeys_fn(t0, n), writes=[("ps", a)])
                    P.op("act", lambda: A_.activation(out=sq[j][:, 0:n], in_=bank(a)[:, 0:n], func=AF.Square),
                         reads=[("ps", a)], writes=[(lnp, "sq", j)])
                    P.op("act", lambda: A_.activation(out=raw[j][:, 0:n], in_=bank(a)[:, 0:n], func=AF.Copy),
                         reads=[("ps", a)], writes=[(lnp, "raw", j)])
                    return (a, j)

                def s2(st, si=si, t0=t0, n=n):
                    a, j = st
                    b = single.next()
                    P.op("pe", lambda: PE.matmul(bank(b)[:, 0:n], lhsT=onesb[:], rhs=sq[j][:, 0:n], start=True, stop=True),
                         reads=[(lnp, "sq", j), "onesb"], writes=[("ps", b)])
                    P.op("act", lambda: A_.activation(out=lnb[j][:, 0:n], in_=bank(b)[:, 0:n], func=AF.Ln, bias=epsb[:, 0:1], scale=1.0),
                         reads=[("ps", b), "epsb"], writes=[(lnp, "ln", j)])
                    P.op("act", lambda: A_.activation(out=lnb[j][:, 0:n], in_=lnb[j][:, 0:n], func=AF.Exp, scale=-0.5),
                         reads=[(lnp, "ln", j)], writes=[(lnp, "ln", j)])
                    return st

                def s3(st, si=si, t0=t0, n=n):
                    a, j = st
                    P.op("dve", lambda: V_.scalar_tensor_tensor(out=dst_fn(si, n), in0=raw[j][:, 0:n], scalar=gain_ap, in1=lnb[j][:, 0:n],
                                                                op0=ALU.mult, op1=ALU.mult),
                         reads=[(lnp, "raw", j), (lnp, "ln", j), "vecs", "gqs"], writes=dkeys_fn(si))
                units.append((s1, s2, s3))

        def hkeys(t0, n):
            return [("hT", t) for t in range(t0 // 128, (t0 + n + 127) // 128)]

        def gate_u(wt, wkey, col0, seg, th, u_dst, ukeys, rot=None):
            a = (rot or single).next()
            t0 = HALO + seg * 512
            j = wstate.setdefault("gu", 0) % 2
            wstate["gu"] += 1
            for kc in range(NKC):
                P.op("pe", (lambda kc=kc: PE.matmul(bank(a), lhsT=wt[:, kc, col0:col0 + 128], rhs=hT[:, kc, t0:t0 + 512],
                                                    start=(kc == 0), stop=(kc == NKC - 1))),
                     reads=[wkey] + hkeys(t0, 512), writes=[("ps", a)])
            P.op("act", lambda: A_.activation(out=th[j], in_=bank(a), func=AF.Tanh, scale=0.5),
                 reads=[("ps", a)], writes=[("A", "th", j)])
            P.op("dve", lambda: V_.scalar_tensor_tensor(out=u_dst, in0=th[j], scalar=1.0, in1=bank(a), op0=ALU.add, op1=ALU.mult),
                 reads=[("A", "th", j), ("ps", a)], writes=ukeys)

        interleave(phase0_pieces(xs, 0, SL // 128, gain_d, hT, "hT", big=True), [])
        interleave(phase0_pieces(mem, 0, 2, mgain_d, memT, "yp"), [])
        fence()
        ar = Arena()
        sq = [ar.bf16(512) for _ in range(3)]
        lnb = [ar.f32(512) for _ in range(3)]
        raw = [ar.f32(512) for _ in range(3)]
        MKEYS = [("yp", i) for i in range(4)]
        wt, wkey = wnext("memk")
        units = []
        for h in range(4):
            norm_units(wt, wkey, h * 128, memT, (lambda t0, n: MKEYS), [(0, 256)], vecs[:, 20 + h:21 + h],
                       (lambda si, n, h=h: mk_n[:, h, :]), (lambda si, h=h: [("mk", h)]), sq, lnb, units, raw=raw)
        pipeline2(units)
        wt, wkey = wnext("memv")
        for tt in range(2):
            a = single.next()
            for kc in range(NKC):
                P.op("pe", (lambda kc=kc, a=a, tt=tt, wt=wt: PE.matmul(bank(a), lhsT=memT[:, kc, tt * 128:(tt + 1) * 128], rhs=wt[:, kc, :],
                                                                       start=(kc == 0), stop=(kc == NKC - 1))),
                     reads=[wkey] + MKEYS, writes=[("ps", a)])
            P.op("act", (lambda a=a, tt=tt: A_.activation(out=mv[:, tt, :], in_=bank(a), func=AF.Copy)),
                 reads=[("ps", a)], writes=[("mv", tt)])

        for p in range(NPASS):

            fence()
            fence2()
            fence3()
            ar = Arena()
            ar2 = Arena(arena2)
            sets = [dict(qn=ar.bf16(T), kn=ar.bf16(SL), Vt=ar.bf16(SL), u=ar.f32(T), i=i) for i in range(2)]
            rec = [ar.f32(128) for _ in range(2)]
            tq = [ar.f32(128) for _ in range(2)]
            th = [ar.f32(512) for _ in range(2)]
            lnb = [ar2.f32(512) for _ in range(3)]
            raw = [ar2.f32(512) for _ in range(3)]
            sq = [ar2.bf16(512) for _ in range(3)]
            tmpS = [ar2.f32(640) for _ in range(3)]
            PT = [ar2.bf16(640) for _ in range(3)]

            def proj_pieces(h, S, wt, wkey):
                si_ = S["i"]
                qn, kn, Vt, u = S["qn"], S["kn"], S["Vt"], S["u"]
                bs = h % 2
                koff = 0 if p == 0 else 1

                def p0():
                    P.op("sp", lambda: SP.dma_start(out=Bh[bs][:], in_=relb_d[h, :, :]), writes=[("Bh", bs)], dma_key=("Bh", bs))
                    P.op("pool", lambda: G_.memset(Bh[bs][64:128, 0:64], NEG), reads=[("Bh", bs)], writes=[("Bh", bs)])
                    P.op("pool", lambda: G_.memset(Bh[bs][0:64, 576:640], NEG), reads=[("Bh", bs)], writes=[("Bh", bs)])
                    P.op("act", lambda: A_.activation(out=Bh[bs][:], in_=Bh[bs][:], func=AF.Exp), reads=[("Bh", bs)], writes=[("Bh", bs)])
                    if p > 0:
                        P.op("sp", lambda: SP.dma_start(out=kn[:, 0:HALO], in_=kvs_k[h, :, :]), reads=[("kvs", h)],
                             writes=[("A", "kn", si_, 0)], dma_key=("kvl", si_))
                        P.op("sp", lambda: SP.dma_start(out=Vt[:, 0:HALO], in_=kvs_v[h, :, :]), reads=[("kvs", h)],
                             writes=[("A", "V", si_, 0)], dma_key=("kvl", si_))
                units = []
                ksegs = [(0, 512), (512, 512), (1024, 512)]
                norm_units(wt, wkey, 128, hT, hkeys, ksegs if p == 0 else ksegs[1:], vecs[:, 8 + h:9 + h],
                           (lambda si, n: kn[:, (si + koff) * 512:(si + koff) * 512 + n]),
                           (lambda si: [("A", "kn", si_, si + koff)]), sq, lnb, units, lnp="B", raw=raw)
                norm_units(wt, wkey, 0, hT, hkeys, [(HALO, 512), (HALO + 512, 512)], gqs[:, h:h + 1],
                           (lambda si, n: qn[:, si * 512:si * 512 + n]), (lambda si: [("A", "qn", si_, si)]), sq, lnb, units, lnp="B", raw=raw)
                pcs = [p0] + pipeline2_pieces(units)

                def vgroup(tg):
                    a = single.next()
                    for t4 in range(4):
                        tt = tg * 4 + t4
                        for kc in range(NKC):
                            P.op("pe", (lambda kc=kc, t4=t4, tt=tt: PE.matmul(
                                bank(a)[:, t4 * 128:(t4 + 1) * 128], lhsT=hT[:, kc, tt * 128:(tt + 1) * 128], rhs=wt[:, kc, 256:384],
                                start=(kc == 0), stop=(kc == NKC - 1))),
                                reads=[wkey, ("hT", tt)], writes=[("ps", a)])
                    P.op("act", lambda: A_.activation(out=Vt[:, tg * 512:(tg + 1) * 512], in_=bank(a), func=AF.Copy),
                         reads=[("ps", a)], writes=[("A", "V", si_, tg)])
                for tg in range(koff, 3):
                    pcs.append(lambda tg=tg: vgroup(tg))
                for seg in range(2):
                    pcs.append(lambda seg=seg: gate_u(wt, wkey, 384, seg, th, u[:, seg * 512:(seg + 1) * 512], [("A", "u", si_, seg)]))
                if p + 1 < NPASS:
                    def save():
                        P.op("sp", lambda: SP.dma_start(out=kvs_k[h, :, :], in_=kn[:, T:T + HALO]), reads=[("A", "kn", si_, 2)],
                             writes=[("kvs", h)], dma_key="kvs_st")
                        P.op("sp", lambda: SP.dma_start(out=kvs_v[h, :, :], in_=Vt[:, T:T + HALO]), reads=[("A", "V", si_, 2)],
                             writes=[("kvs", h)], dma_key="kvs_st")
                    pcs.append(save)
                return pcs

            def attn_pieces(h, S):
                si_ = S["i"]
                qn, kn, Vt, u = S["qn"], S["kn"], S["Vt"], S["u"]
                bs = h % 2
                units = []
                for r in range(8):
                    def s1(r=r):
                        b0 = pairs.next()
                        pa = pair(b0)
                        i = r % 3
                        for m in range(5):
                            kt = r + 4 - m
                            P.op("pe", (lambda m=m, kt=kt: PE.matmul(pa[:, m * 128:(m + 1) * 128], lhsT=kn[:, kt * 128:(kt + 1) * 128],
                                                                      rhs=qn[:, r * 128:(r + 1) * 128], start=True, stop=True)),
                                 reads=[("A", "kn", si_, kt // 4), ("A", "qn", si_, r // 4)], writes=[("ps", b0 if m < 4 else b0 + 1)])
                        if p == 0 and r < 4:
                            c = (r + 1) * 128
                            P.op("act", lambda: A_.activation(out=tmpS[i][:, 0:c], in_=pa[:, 0:c], func=AF.Exp),
                                 reads=[("ps", b0), ("ps", b0 + 1)], writes=[("B", "tmpS", i)])
                            P.op("act", lambda: A_.activation(out=tmpS[i][:, c:640], in_=pa[:, c:640], func=AF.Exp, bias=kmask[:, 0:1], scale=1.0),
                                 reads=[("ps", b0), ("ps", b0 + 1), "kmask"], writes=[("B", "tmpS", i)])
                        else:
                            P.op("act", lambda: A_.activation(out=tmpS[i], in_=pa[:, 0:640], func=AF.Exp),
                                 reads=[("ps", b0), ("ps", b0 + 1)], writes=[("B", "tmpS", i)])
                        return (b0, i)

                    def s2(st, r=r):
                        b0, i = st
                        P.op("pool", lambda: G_.tensor_tensor(out=PT[i], in0=tmpS[i], in1=Bh[bs][:], op=ALU.mult),
                             reads=[("B", "tmpS", i), ("Bh", bs)], writes=[("B", "PT", i)])

                    def s3(st, r=r):
                        b0, i = st
                        ri = r % 2
                        c = single.next()
                        pc = bank(c)
                        for m in range(5):
                            kt = r + 4 - m
                            P.op("pe", (lambda m=m, kt=kt: PE.matmul(pc[:, 0:128], lhsT=Vt[:, kt * 128:(kt + 1) * 128],
                                                                      rhs=PT[i][:, m * 128:(m + 1) * 128], start=(m == 0), stop=(m == 4))),
                                 reads=[("A", "V", si_, kt // 4), ("B", "PT", i)], writes=[("ps", c)])
                        for m in range(5):
                            P.op("pe", (lambda m=m: PE.matmul(pc[:, 128:256], lhsT=ones1[:], rhs=PT[i][:, m * 128:(m + 1) * 128],
                                                               start=(m == 0), stop=(m == 4))),
                                 reads=["ones1", ("B", "PT", i)], writes=[("ps", c)])
                        P.op("dve", lambda: V_.reciprocal(out=rec[ri], in_=pc[:, 128:256]), reads=[("ps", c)], writes=[("A", "rec", ri)])
                        P.op("dve", lambda: V_.tensor_tensor(out=tq[ri], in0=pc[:, 0:128], in1=rec[ri], op=ALU.mult),
                             reads=[("ps", c), ("A", "rec", ri)], writes=[("A", "tq", ri)])
                        P.op("dve", lambda: V_.scalar_tensor_tensor(out=o_sb[:, h, r * 128:(r + 1) * 128], in0=tq[ri], scalar=0.5,
                                                                    in1=u[:, r * 128:(r + 1) * 128], op0=ALU.mult, op1=ALU.mult),
                             reads=[("A", "tq", ri), ("A", "u", si_, r // 4)], writes=[("o", h, r // 4)])
                    units.append((s1, s2, s3))
                return pipeline2_pieces(units)

            prev_attn = []
            for h in range(8):
                wt, wkey = wnext("head%d" % h)
                interleave(prev_attn, proj_pieces(h, sets[h % 2], wt, wkey))
                prev_attn = attn_pieces(h, sets[h % 2])
            interleave(prev_attn, [])

            fence()
            ar = Arena()
            va = ar.f32(1152)
            X = ar.f32(1152)
            Y = ar.f32(1152)
            pooled = [ar.bf16(T) for _ in range(4)]
            th = [ar.f32(512) for _ in range(2)]
            ub = [ar.f32(512) for _ in range(2)]
            wt, wkey = wnext("vb")
            for g in range(4):
                for (t0, n, c0) in ((384, 128, 0), (512, 512, 128), (1024, 512, 640)):
                    a = single.next()
                    for kc in range(NKC):
                        P.op("pe", (lambda kc=kc, a=a, t0=t0, n=n, g=g, wt=wt: PE.matmul(
                            bank(a)[:, 0:n], lhsT=wt[:, kc, g * 128:(g + 1) * 128], rhs=hT[:, kc, t0:t0 + n],
                            start=(kc == 0), stop=(kc == NKC - 1))),
                            reads=[wkey] + hkeys(t0, n), writes=[("ps", a)])
                    P.op("act", (lambda a=a, n=n, c0=c0: A_.activation(out=va[:, c0:c0 + n], in_=bank(a)[:, 0:n], func=AF.Copy)),
                         reads=[("ps", a)], writes=[("A", "va")])
                srcb, k, sh = va, "va", 1
                bufs = [(X, "X"), (Y, "Y")]
                for lvl in range(g + 1):
                    dstb, dk = bufs[lvl % 2]
                    lo = 2 * sh - 1
                    P.op("pool", (lambda dstb=dstb, srcb=srcb, lo=lo, sh=sh: G_.tensor_tensor(
                        out=dstb[:, lo:1152], in0=srcb[:, lo:1152], in1=srcb[:, lo - sh:1152 - sh], op=ALU.add)),
                        reads=[("A", k)], writes=[("A", dk)])
                    srcb, k, sh = dstb, dk, sh * 2
                if p == 0:
                    P.op("pool", (lambda srcb=srcb, g=g: G_.tensor_tensor(out=srcb[:, 128:144], in0=srcb[:, 128:144],
                                                                          in1=pcorr[:, g * 16:(g + 1) * 16], op=ALU.mult)),
                         reads=[("A", k), "pcorr"], writes=[("A", k)])
                P.op("dve", (lambda srcb=srcb, g=g: V_.scalar_tensor_tensor(out=pooled[g], in0=srcb[:, 128:1152], scalar=1.0 / POOL_W[g],
                                                                            in1=va[:, 128:1152], op0=ALU.mult, op1=ALU.subtract)),
                     reads=[("A", k), ("A", "va")], writes=[("A", "pooled", g)])
            wt, wkey = wnext("gb")
            rot_b = Rot([4, 5, 6, 7])
            for g in range(4):
                for seg in range(2):
                    j = wstate.setdefault("gu", 0) % 2
                    gate_u(wt, wkey, g * 128, seg, th, ub[j], [("A", "ub", j)], rot=rot_b)
                    a = rot_b.next()
                    P.op("pe", (lambda a=a, g=g, seg=seg: PE.matmul(bank(a), lhsT=poolw[:, g, :], rhs=pooled[g][:, seg * 512:(seg + 1) * 512],
                                                                    start=True, stop=True)),
                         reads=["poolw", ("A", "pooled", g)], writes=[("ps", a)])
                    P.op("dve", (lambda a=a, g=g, seg=seg, j=j: V_.scalar_tensor_tensor(
                        out=o_sb[:, 8 + g, seg * 512:(seg + 1) * 512], in0=bank(a), scalar=psc[:, g:g + 1], in1=ub[j],
                        op0=ALU.mult, op1=ALU.mult)),
                        reads=[("ps", a), "psc", ("A", "ub", j)], writes=[("o", 8 + g, seg)])

            fence()
            fence2()
            ar = Arena()
            ar2 = Arena(arena2)
            qn4 = [ar.bf16(T) for _ in range(4)]
            th = [ar.f32(512) for _ in range(2)]
            ub = [ar.f32(512) for _ in range(2)]
            PT2 = [ar.bf16(1024) for _ in range(2)]
            rec2 = [ar.f32(512) for _ in range(2)]
            sq = [ar2.bf16(512) for _ in range(3)]
            lnb = [ar2.f32(512) for _ in range(3)]
            raw = [ar2.f32(512) for _ in range(3)]
            wt, wkey = wnext("qm")
            units = []
            for h in range(4):
                norm_units(wt, wkey, h * 128, hT, hkeys, [(HALO, 512), (HALO + 512, 512)], gqs[:, 8 + h:9 + h],
                           (lambda si, n, h=h: qn4[h][:, si * 512:si * 512 + n]), (lambda si, h=h: [("A", "qn4", h, si)]), sq, lnb, units, lnp="B", raw=raw)
            pipeline2(units)
            wt, wkey = wnext("gm")
            units = []
            for h in range(4):
                for seg in range(2):
                    def s1(h=h, seg=seg, wt=wt, wkey=wkey):
                        j = wstate.setdefault("gu", 0) % 2
                        gate_u(wt, wkey, h * 128, seg, th, ub[j], [("A", "ub", j)])
                        b0 = pairs.next()
                        pa = pair(b0)
                        for jj in range(2):
                            P.op("pe", (lambda jj=jj: PE.matmul(pa[:, jj * 512:(jj + 1) * 512], lhsT=mk_n[:, h, jj * 128:(jj + 1) * 128],
                                                                rhs=qn4[h][:, seg * 512:(seg + 1) * 512], start=True, stop=True)),
                                 reads=[("mk", h), ("A", "qn4", h, seg)], writes=[("ps", b0 + jj)])
                        P.op("act", lambda: A_.activation(out=PT2[j], in_=pa, func=AF.Exp),
                             reads=[("ps", b0), ("ps", b0 + 1)], writes=[("A", "PT2", j)])
                        return j

                    def s2(j, h=h, seg=seg):
                        c = single.next()
                        for jj in range(2):
                            P.op("pe", (lambda jj=jj: PE.matmul(bank(c), lhsT=mv[:, jj, h * 128:(h + 1) * 128],
                                                                rhs=PT2[j][:, jj * 512:(jj + 1) * 512], start=(jj == 0), stop=(jj == 1))),
                                 reads=[("mv", jj), ("A", "PT2", j)], writes=[("ps", c)])
                        d = single.next()
                        for jj in range(2):
                            P.op("pe", (lambda jj=jj: PE.matmul(bank(d), lhsT=ones2[:], rhs=PT2[j][:, jj * 512:(jj + 1) * 512],
                                                                start=(jj == 0), stop=(jj == 1))),
                                 reads=["ones2", ("A", "PT2", j)], writes=[("ps", d)])
                        P.op("dve", lambda: V_.reciprocal(out=rec2[j], in_=bank(d)), reads=[("ps", d)], writes=[("A", "rec2", j)])
                        P.op("dve", lambda: V_.tensor_tensor(out=rec2[j], in0=bank(c), in1=rec2[j], op=ALU.mult),
                             reads=[("ps", c), ("A", "rec2", j)], writes=[("A", "rec2", j)])
                        P.op("pool", lambda: G_.tensor_tensor(out=o_sb[:, 12 + h, seg * 512:(seg + 1) * 512], in0=rec2[j], in1=ub[j], op=ALU.mult),
                             reads=[("A", "rec2", j), ("A", "ub", j)], writes=[("o", 12 + h, seg)])
                    units.append((s1, s2))
            pipeline2(units)

            fence()
            fence2()
            ar = Arena()
            thg = [[ar.f32(512) for _ in range(3)] for _ in range(2)]
            acc = [ar.f32(512) for _ in range(3)]
            ssum = ar.f32(512)
            xres = [ar.f32(512) for _ in range(3)]
            res = [ar.f32(512) for _ in range(2)]
            allb = Rot([4, 5, 6, 7, 0, 1, 2, 3])
            KB = [(0, 8), (8, 12), (12, 16)]
            for db in range(16):
                wt, wkey = wnext("mg%d" % db)
                for half in range(2):
                    j = (db * 2 + half) % 2
                    t0 = HALO + half * 512
                    zb = []
                    for br in range(3):
                        a = allb.next()
                        zb.append(a)
                        for kc in range(NKC):
                            P.op("pe", (lambda kc=kc, a=a, br=br, wt=wt, t0=t0: PE.matmul(
                                bank(a), lhsT=wt[:, kc, br * 128:(br + 1) * 128], rhs=hT[:, kc, t0:t0 + 512],
                                start=(kc == 0), stop=(kc == NKC - 1))),
                                reads=[wkey] + hkeys(t0, 512), writes=[("ps", a)])
                        P.op("act", (lambda a=a, br=br, j=j, db=db: A_.activation(out=thg[j][br], in_=bank(a), func=AF.Tanh,
                                                                                  bias=bmh[:, br * 16 + db:br * 16 + db + 1], scale=0.5)),
                             reads=[("ps", a), "bmh"], writes=[("A", "thg", j, br)])
                    for br in range(3):
                        a = allb.next()
                        k0, k1 = KB[br]
                        for kc in range(k0, k1):
                            P.op("pe", (lambda kc=kc, a=a, wt=wt, half=half, k0=k0, k1=k1: PE.matmul(
                                bank(a), lhsT=wt[:, kc, 384:512], rhs=o_sb[:, kc, half * 512:(half + 1) * 512],
                                start=(kc == k0), stop=(kc == k1 - 1))),
                                reads=[wkey, ("o", kc, half)], writes=[("ps", a)])
                        P.op("dve", (lambda a=a, br=br, j=j: V_.scalar_tensor_tensor(out=acc[br], in0=thg[j][br], scalar=1.0, in1=bank(a),
                                                                                     op0=ALU.add, op1=ALU.mult)),
                             reads=[("A", "thg", j, br), ("ps", a)], writes=[("A", "acc", br)])
                    P.op("pool", lambda: G_.tensor_tensor(out=ssum, in0=acc[0], in1=acc[1], op=ALU.add),
                         reads=[("A", "acc", 0), ("A", "acc", 1)], writes=[("A", "ssum")])
                    P.op("pool", (lambda db=db, half=half: G_.tensor_tensor(out=yp[:, db, half * 512:(half + 1) * 512], in0=ssum, in1=acc[2], op=ALU.add)),
                         reads=[("A", "ssum"), ("A", "acc", 2)], writes=[("yp", db)])

            def xload(un):
                c_, tt_ = un // 8, un % 8
                jx = un % 3
                r0 = HALO + p * T + tt_ * 128
                P.op("act", lambda: A_.dma_start(out=xres[jx], in_=xs[r0:r0 + 128, c_ * 512:(c_ + 1) * 512]),
                     writes=[("A", "xres", jx)], dma_key=("xres", jx))
            xload(0)
            p3 = []
            wcur = {}
            for c in range(4):
                for tt in range(8):
                    def unit(c=c, tt=tt):
                        if tt == 0:
                            wcur["w"] = wnext("wo%d" % c)
                        wt, wkey = wcur["w"]
                        un = c * 8 + tt
                        j = un % 2
                        jx = un % 3
                        a = single.next()
                        if un + 1 < 32:
                            xload(un + 1)
                        for kc in range(NKC):
                            P.op("pe", (lambda kc=kc: PE.matmul(bank(a), lhsT=yp[:, kc, tt * 128:(tt + 1) * 128], rhs=wt[:, kc, :],
                                                                start=(kc == 0), stop=(kc == NKC - 1))),
                                 reads=[wkey, ("yp", kc)], writes=[("ps", a)])
                        P.op("dve", lambda: V_.scalar_tensor_tensor(out=res[j], in0=bank(a), scalar=0.5, in1=xres[jx],
                                                                    op0=ALU.mult, op1=ALU.add),
                             reads=[("ps", a), ("A", "xres", jx)], writes=[("A", "res", j)])
                        orow = p * T + tt * 128
                        P.op("sp", lambda: SP.dma_start(out=out_d[orow:orow + 128, c * 512:(c + 1) * 512], in_=res[j]),
                             reads=[("A", "res", j)], dma_key=("out", j))
                    p3.append(unit)
            if p + 1 < NPASS:
                interleave(p3, phase0_pieces(xs, (p + 1) * T, SL // 128, gain_d, hT, "hT", tile0=HALO // 128, xq="act"))
            else:
                interleave(p3, [])

        P.emit()
    return nc


_CACHE = {}


def _program():
    if "nc" not in _CACHE:
        _CACHE["nc"] = build_program()
    return _CACHE["nc"]


def kernel(x, mem, norm_gain, mem_norm_gain, w_in, w_merge, b_merge, a_q_gain, a_k_gain,
           a_rel_bias, pool_w, pool_scale, w_mem_kv, m_q_gain, m_k_gain,
           w_branch_a, w_branch_b, w_branch_m, w_out):
    f = lambda a: np.ascontiguousarray(np.asarray(a, dtype=np.float32))
    x = f(x); mem = f(mem)
    B, S, _ = x.shape
    half_len = S // 2
    gain_bc = f(np.broadcast_to(np.asarray(norm_gain)[0][None, :], (128, D)))
    mgain_bc = f(np.broadcast_to(np.asarray(mem_norm_gain)[0][None, :], (128, D)))
    vecs = f(np.concatenate([
        np.asarray(a_q_gain)[0].T, np.asarray(a_k_gain)[0].T, np.asarray(m_q_gain)[0].T, np.asarray(m_k_gain)[0].T,
        np.asarray(pool_scale)[0].reshape(4, 128).T, np.asarray(b_merge)[0].reshape(48, 128).T], axis=1))
    ii = np.arange(128)[:, None]
    jj = np.arange(640)[None, :]
    idx = np.clip(jj - ii, -256, 256) + 256
    relb = f(np.asarray(a_rel_bias)[0][:, idx])
    ident = np.eye(128, dtype=np.float32)
    wstream = build_wstream(dict(
        w_in=np.asarray(w_in)[0], w_merge=np.asarray(w_merge)[0], w_mem_kv=np.asarray(w_mem_kv)[0],
        w_a=np.asarray(w_branch_a)[0], w_b=np.asarray(w_branch_b)[0], w_m=np.asarray(w_branch_m)[0],
        w_out=np.asarray(w_out)[0]))
    shared = dict(gain_bc=gain_bc, mgain_bc=mgain_bc, wstream=wstream, vecs=vecs, relb=relb,
                  pool_w=f(np.asarray(pool_w)[0]), ident=ident)
    in_maps = []
    tt = np.arange(16)
    for c in range(N_CORES):
        b, hf = c // 2, c % 2
        xs = np.zeros((HALO + half_len, D), np.float32)
        if hf == 0:
            xs[HALO:] = x[b, 0:half_len]
        else:
            xs[:] = x[b, half_len - HALO:S]
        km = np.full((128, 1), NEG if hf == 0 else 0.0, np.float32)
        pc = np.ones((128, 64), np.float32)
        if hf == 0:
            for g, w in enumerate(POOL_W):
                pc[:, g * 16:(g + 1) * 16] = (w / np.minimum(tt + 1, w)).astype(np.float32)[None, :]
        m = dict(shared)
        m.update(xs=xs, mem=mem[b], kmask=km, pcorr=pc)
        in_maps.append(m)
    nc = _program()
    r = run_bass_kernel_spmd(nc, in_maps, core_ids=list(range(N_CORES)))
    out = np.empty((B, S, D), np.float32)
    for c in range(N_CORES):
        b, hf = c // 2, c % 2
        out[b, hf * half_len:(hf + 1) * half_len] = r.results[c]["out"]
    return out
```

```python
import contextlib
import numpy as np
import concourse.bass as bass
import concourse.mybir as mybir
from concourse.bass_utils import run_bass_kernel_spmd

F32 = mybir.dt.float32
BF16 = mybir.dt.bfloat16
AF = mybir.ActivationFunctionType
ALU = mybir.AluOpType

D = 2048
T = 1024
HALO = 512
SL = T + HALO
NPASS = 2
NKC = 16
NSLOT = 3
EPS = 1e-6
NEG = -30000.0
SCALE = float(128 ** -0.5)
S_HI = float(np.float32(SCALE))
S_LO = float(np.float32(SCALE - S_HI))
SQ_SCALE = 2.0 ** -5
LN_SCALE = 0.5
assert abs(SQ_SCALE * SQ_SCALE * LN_SCALE * D - 1.0) < 1e-15
POOL_W = (2, 4, 8, 16)
N_CORES = 8


class _Op:
    __slots__ = ("eng", "fn", "dma_key", "needs_inc", "count", "waits", "idx")


class Prog:
    def __init__(self, nc):
        self.nc = nc
        self.eng = dict(pe=nc.tensor, act=nc.scalar, dve=nc.vector, pool=nc.gpsimd, sp=nc.sync)
        self.ops = []
        self.res = {}
        self.dma_cnt = {}

    def op(self, eng, fn, reads=(), writes=(), dma_key=None):
        reads = list(reads)
        writes = list(writes)
        if any(isinstance(k, tuple) and k[0] == "A" for k in reads + writes):
            reads.append("arena")
        if any(isinstance(k, tuple) and k[0] == "B" for k in reads + writes):
            reads.append("arena2")
        if any(isinstance(k, tuple) and k[0] == "C" for k in reads + writes):
            reads.append("arena3")
        o = _Op()
        o.eng = eng
        o.fn = fn()
        o.dma_key = dma_key
        o.needs_inc = False
        o.count = None
        o.idx = len(self.ops)
        deps = []
        for k in reads:
            r = self.res.get(k)
            if r is not None and r[0] is not None:
                deps.append(r[0])
        for k in writes:
            r = self.res.get(k)
            if r is not None:
                if r[0] is not None:
                    deps.append(r[0])
                deps.extend(r[1])
        for k in reads:
            r = self.res.get(k)
            if r is None:
                r = self.res[k] = [None, []]
            r[1].append(o)
        for k in writes:
            self.res[k] = [o, []]
        waits = {}
        for d in deps:
            if d is o:
                continue
            if d.dma_key is not None:
                key = ("dma", d.dma_key)
                v = 16 * self.dma_cnt[d.dma_key]
                if waits.get(key, (0, None))[0] < v:
                    waits[key] = (v, None)
            else:
                if d.eng == eng and eng == "pe":
                    continue
                d.needs_inc = True
                key = ("eng", d.eng)
                cur = waits.get(key)
                if cur is None or cur[1].idx < d.idx:
                    waits[key] = (None, d)
        o.waits = waits
        if dma_key is not None:
            self.dma_cnt[dma_key] = self.dma_cnt.get(dma_key, 0) + 1
        self.ops.append(o)
        return o

    def emit(self):
        nc = self.nc
        sems = {}
        with contextlib.ExitStack() as es:
            for e in self.eng:
                sems[("eng", e)] = es.enter_context(nc.semaphore("s_" + e))
            for i, k in enumerate(self.dma_cnt):
                sems[("dma", k)] = es.enter_context(nc.semaphore("d_%d" % i))
            cnt = {e: 0 for e in self.eng}
            waited = {e: {} for e in self.eng}
            for o in self.ops:
                E = self.eng[o.eng]
                for key, (v, d) in o.waits.items():
                    if d is not None:
                        v = d.count
                    if waited[o.eng].get(key, 0) >= v:
                        continue
                    waited[o.eng][key] = v
                    E.wait_ge(sems[key], v)
                m_, a_, k_ = o.fn
                ins = m_(*a_, **k_)
                if o.dma_key is not None:
                    ins.then_inc(sems[("dma", o.dma_key)], 16)
                elif o.needs_inc:
                    cnt[o.eng] += 1
                    o.count = cnt[o.eng]
                    ins.then_inc(sems[("eng", o.eng)], 1)
            E = self.eng["sp"]
            for k, n in self.dma_cnt.items():
                E.wait_ge(sems[("dma", k)], 16 * n)
            for e in self.eng:
                if cnt[e] > 0:
                    E.wait_ge(sems[("eng", e)], cnt[e])


class _Rec:
    def __init__(self, eng):
        self._e = eng

    def __getattr__(self, name):
        m = getattr(self._e, name)
        return lambda *a, **k: (m, a, k)


class Rot:
    def __init__(self, items):
        self.items = list(items)
        self.i = 0

    def next(self):
        v = self.items[self.i % len(self.items)]
        self.i += 1
        return v


def chunk_specs():
    pro = [("memk", [("w_mem_kv", 0, 2048, 0, 512, 0, 0)]),
           ("memv", [("w_mem_kv", 0, 2048, 512, 512, 0, 0)])]
    per = []
    for h in range(8):
        per.append(("head%d" % h, [("w_in", 0, 2048, j * 1024 + h * 128, 128, 0, j * 128) for j in range(4)]))
    per.append(("vb", [("w_in", 0, 2048, 4096, 512, 0, 0)]))
    per.append(("gb", [("w_in", 0, 2048, 4608, 512, 0, 0)]))
    per.append(("qm", [("w_in", 0, 2048, 5120, 512, 0, 0)]))
    per.append(("gm", [("w_in", 0, 2048, 5632, 512, 0, 0)]))
    for db in range(16):
        pcs = [("w_merge", 0, 2048, br * 2048 + db * 128, 128, 0, br * 128) for br in range(3)]
        pcs.append(("w_a", 0, 1024, db * 128, 128, 0, 384))
        pcs.append(("w_b", 0, 512, db * 128, 128, 1024, 384))
        pcs.append(("w_m", 0, 512, db * 128, 128, 1536, 384))
        per.append(("mg%d" % db, pcs))
    for c in range(4):
        per.append(("wo%d" % c, [("w_out", 0, 2048, c * 512, 512, 0, 0)]))
    return pro, per


def build_wstream(ws):
    pro, per = chunk_specs()
    allc = pro + per
    out = np.empty((len(allc), 128, NKC * 512), np.float32)
    for n, (tag, pcs) in enumerate(allc):
        m = np.empty((2048, 512), np.float32)
        for (src, r0, nr, c0, ncol, dr, dc) in pcs:
            m[dr:dr + nr, dc:dc + ncol] = ws[src][r0:r0 + nr, c0:c0 + ncol]
        out[n] = m.reshape(NKC, 128, 512).transpose(1, 0, 2).reshape(128, NKC * 512)
    return out


def build_program():
    nc = bass.Bass("TRN2", target_bir_lowering=False)

    def din(name, shape):
        return nc.dram_tensor(name, list(shape), F32, kind="ExternalInput").ap()

    xs = din("xs", [HALO + NPASS * T, D])
    mem = din("mem", [256, D])
    gain_d = din("gain_bc", [128, D])
    mgain_d = din("mgain_bc", [128, D])
    _pro, _per = chunk_specs()
    wstream_d = din("wstream", [len(_pro) + len(_per), 128, NKC * 512])
    vecs_d = din("vecs", [128, 76])
    relb_d = din("relb", [8, 128, 640])
    poolw_d = din("pool_w", [4, 128, 128])
    ident_d = din("ident", [128, 128])
    kmask_d = din("kmask", [128, 1])
    pcorr_d = din("pcorr", [128, 64])
    out_d = nc.dram_tensor("out", [NPASS * T, D], F32, kind="ExternalOutput").ap()
    kvs_k = nc.dram_tensor("kvs_k", [8, 128, HALO], BF16, kind="Internal").ap()
    kvs_v = nc.dram_tensor("kvs_v", [8, 128, HALO], BF16, kind="Internal").ap()

    es = contextlib.ExitStack()
    with es:
        def sb(name, shape, dt):
            return es.enter_context(nc.sbuf_tensor("sb_" + name, list(shape), dt))

        hT = sb("hT", [128, NKC, SL], BF16)
        o_flat = sb("o_sb", [128, 16 * T], BF16)
        o_sb = o_flat[:, :].rearrange("p (k t) -> p k t", k=16)
        yp_flat = sb("yp", [128, 16 * T], BF16)
        yp = yp_flat[:, :].rearrange("p (k t) -> p k t", k=16)
        memT = yp_flat[:, 0:NKC * 256].rearrange("p (k t) -> p k t", k=NKC)
        wsl = [sb("wsl%d" % i, [128, NKC, 512], BF16) for i in range(NSLOT)]
        vecs = sb("vecs", [128, 76], F32)
        bmh = sb("bmh", [128, 48], F32)
        gqs = sb("gqs", [128, 12], F32)
        gqt = sb("gqt", [128, 12], F32)
        psc = sb("psc", [128, 4], F32)
        ident_f = sb("ident_f", [128, 128], F32)
        ident_b = sb("ident_b", [128, 128], BF16)
        onesb = sb("onesb", [128, 128], BF16)
        ones1 = sb("ones1", [128, 128], BF16)
        ones2 = sb("ones2", [128, 128], BF16)
        epsb = sb("epsb", [128, 1], F32)
        kmask = sb("kmask", [128, 1], F32)
        pcorr = sb("pcorr", [128, 64], F32)
        poolw = sb("poolw", [128, 4, 128], BF16)
        mk_n = sb("mk_n", [128, 4, 256], BF16)
        mv = sb("mv", [128, 2, 512], BF16)
        Bh = [sb("Bh%d" % i, [128, 640], F32) for i in range(2)]
        stat = sb("stat", [128, 64], F32)
        dummy = sb("dummy", [128, 1], F32)
        dummy2 = sb("dummy2", [128, 1], F32)
        dummy3 = sb("dummy3", [128, 1], F32)
        AW = 8192
        arena = sb("arena", [128, AW], F32)
        ps = es.enter_context(nc.psum_tensor("ps", [128, 4096], F32))

        P = Prog(nc)
        V_ = _Rec(nc.vector)
        A_ = _Rec(nc.scalar)
        G_ = _Rec(nc.gpsimd)
        PE = _Rec(nc.tensor)
        SP = _Rec(nc.sync)

        arena2 = yp_flat[:, :].bitcast(F32)

        class Arena:
            def __init__(self, base=None):
                self.off = 0
                self.base = arena if base is None else base

            def f32(self, n):
                a = self.base[:, self.off:self.off + n]
                self.off += n
                assert self.off <= AW, self.off
                return a

            def bf16(self, n):
                assert n % 2 == 0
                a = self.base[:, self.off:self.off + n // 2].bitcast(BF16)
                self.off += n // 2
                assert self.off <= AW, self.off
                return a

        def fence():
            P.op("pool", lambda: G_.memset(dummy[:], 0.0), writes=["arena"])

        arena3 = o_flat[:, :].bitcast(F32)

        def fence3():
            P.op("pool", lambda: G_.memset(dummy3[:], 0.0),
                 writes=["arena3"] + [("o", b_, h_) for b_ in range(16) for h_ in range(2)])

        def fence2():
            P.op("pool", lambda: G_.memset(dummy2[:], 0.0), writes=["arena2"] + [("yp", i) for i in range(16)])

        def bank(i):
            return ps[:, i * 512:(i + 1) * 512]

        def pair(i):
            return ps[:, i * 512:(i + 2) * 512]

        chunks = [(t, i) for i, (t, _) in enumerate(_pro)]
        for p_ in range(NPASS):
            chunks += [(t, len(_pro) + i) for i, (t, _) in enumerate(_per)]
        wstate = dict(cur=-1, issued=0)

        def issue_chunk(n):
            s = n % NSLOT
            idx = chunks[n][1]
            P.op("pool", lambda: G_.dma_start(out=wsl[s][:], in_=wstream_d[idx].rearrange("p (k c) -> p k c", k=NKC)),
                 reads=([("hT", 1)] if n < NSLOT else []), writes=[("w", s)], dma_key=("w", s))

        def wnext(tag):
            wstate["cur"] += 1
            n = wstate["cur"]
            assert chunks[n][0] == tag, (chunks[n][0], tag)
            while wstate["issued"] < min(len(chunks), n + NSLOT):
                issue_chunk(wstate["issued"])
                wstate["issued"] += 1
            s = n % NSLOT
            return wsl[s], ("w", s)

        P.op("sp", lambda: SP.dma_start(out=vecs[:], in_=vecs_d[:, :]), writes=["vecs"], dma_key="c")
        P.op("sp", lambda: SP.dma_start(out=ident_f[:], in_=ident_d[:, :]), writes=["ident_f"], dma_key="c")
        P.op("sp", lambda: SP.dma_start(out=kmask[:], in_=kmask_d[:, :]), writes=["kmask"], dma_key="c")
        P.op("sp", lambda: SP.dma_start(out=pcorr[:], in_=pcorr_d[:, :]), writes=["pcorr"], dma_key="c")
        P.op("pool", lambda: G_.dma_start(out=poolw[:], in_=poolw_d.rearrange("g c d -> c g d")), writes=["poolw"], dma_key="c2")
        P.op("pool", lambda: G_.memset(onesb[:], 1.0 / 128), writes=["onesb"])
        P.op("pool", lambda: G_.memset(ones1[:], 1.0), writes=["ones1"])
        P.op("pool", lambda: G_.memset(ones2[:], 2.0), writes=["ones2"])
        P.op("pool", lambda: G_.memset(epsb[:], EPS), writes=["epsb"])
        P.op("dve", lambda: V_.tensor_copy(out=ident_b[:], in_=ident_f[:]), reads=["ident_f"], writes=["ident_b"])
        P.op("dve", lambda: V_.tensor_single_scalar(out=bmh[:], in_=vecs[:, 28:76], scalar=0.5, op=ALU.mult), reads=["vecs"], writes=["bmh"])
        P.op("dve", lambda: V_.tensor_single_scalar(out=psc[:], in_=vecs[:, 24:28], scalar=0.5, op=ALU.mult), reads=["vecs"], writes=["psc"])
        for (c0_, c1_, d0_) in ((0, 8, 0), (16, 20, 8)):
            P.op("dve", (lambda c0_=c0_, c1_=c1_, d0_=d0_: V_.tensor_single_scalar(
                out=gqt[:, d0_:d0_ + (c1_ - c0_)], in_=vecs[:, c0_:c1_], scalar=S_LO, op=ALU.mult)), reads=["vecs"], writes=[("gqt", d0_)])
            P.op("dve", (lambda c0_=c0_, c1_=c1_, d0_=d0_: V_.scalar_tensor_tensor(
                out=gqs[:, d0_:d0_ + (c1_ - c0_)], in0=vecs[:, c0_:c1_], scalar=S_HI, in1=gqt[:, d0_:d0_ + (c1_ - c0_)],
                op0=ALU.mult, op1=ALU.add)), reads=["vecs", ("gqt", d0_)], writes=["gqs"])

        single = Rot([0, 1, 2, 3])
        pairs = Rot([4, 6])

        def pipeline2_pieces(units):
            st = {}
            pcs = []
            n = len(units)

            def mk1(i):
                return lambda: st.__setitem__(i, units[i][0]())

            def mk2(i):
                def f():
                    r = units[i][1](st[i])
                    if r is not None:
                        st[i] = r
                return f

            def mk3(i):
                return lambda: units[i][2](st[i])
            three = n > 0 and len(units[0]) == 3
            for i in range(n):
                pcs.append(mk1(i))
                if i >= 1:
                    pcs.append(mk2(i - 1))
                if three and i >= 2:
                    pcs.append(mk3(i - 2))
            if n:
                pcs.append(mk2(n - 1))
                if three:
                    if n >= 2:
                        pcs.append(mk3(n - 2))
                    pcs.append(mk3(n - 1))
            return pcs

        def interleave(a, b):
            a = list(a)
            b = list(b)
            i = j = 0
            while i < len(a) or j < len(b):
                if i < len(a):
                    a[i]()
                    i += 1
                if j < len(b):
                    b[j]()
                    j += 1

        def pipeline2(units):
            interleave(pipeline2_pieces(units), [])

        def phase0_pieces(src, row0, ntiles, gsrc, dstT, dkey, tile0=0, xq="sp", big=False):
            fence3()
            ar = Arena(arena3)
            if big:
                fence()
                ar_a = Arena()
                xt = [ar.f32(2048) for _ in range(3)] + [ar_a.f32(2048)]
                xk = [("C", "xt", 0), ("C", "xt", 1), ("C", "xt", 2), ("A", "xt", 3)]
                hb = [ar.bf16(2048) for _ in range(2)]
                gbc = ar_a.f32(2048)
                gk = ("A", "gbc")
                junk = ar_a.bf16(2048)
            else:
                xt = [ar.f32(2048) for _ in range(2)]
                xk = [("C", "xt", 0), ("C", "xt", 1)]
                hb = [ar.bf16(2048) for _ in range(2)]
                gbc = ar.f32(2048)
                gk = ("C", "gbc")
            nx = len(xt)

            def pg():
                P.op("sp", lambda: SP.dma_start(out=gbc, in_=gsrc[:, :]), writes=[gk], dma_key="g")
            units = []
            XQ = {"sp": SP, "pool": G_, "act": A_}[xq]
            xkey = {"sp": "xt", "pool": "xtp", "act": "xta"}[xq]

            def xdma(i):
                s = i % nx
                P.op(xq, lambda: XQ.dma_start(out=xt[s], in_=src[row0 + i * 128: row0 + (i + 1) * 128, :]),
                     writes=[xk[s]], dma_key=(xkey, s))
            for i in range(tile0, ntiles):
                def s1(i=i):
                    s = i % nx
                    hs = i % 2
                    if xq == "act":
                        if i == tile0:
                            xdma(i)
                        if i + 1 < ntiles:
                            xdma(i + 1)
                    else:
                        xdma(i)
                    if big:
                        P.op("act", lambda: A_.activation(out=junk, in_=xt[s], func=AF.Square, scale=SQ_SCALE,
                                                          accum_out=stat[:, i:i + 1]),
                             reads=[xk[s], "arena"], writes=[("stat", i)])
                    else:
                        P.op("act", lambda: A_.activation(out=hb[hs], in_=xt[s], func=AF.Square, scale=SQ_SCALE,
                                                          accum_out=stat[:, i:i + 1]),
                             reads=[xk[s]], writes=[("C", "hb", hs), ("stat", i)])
                    P.op("act", lambda: A_.activation(out=stat[:, 16 + i:17 + i], in_=stat[:, i:i + 1], func=AF.Ln, bias=epsb[:, 0:1], scale=LN_SCALE),
                         reads=[("stat", i), "epsb"], writes=[("stat", 16 + i)])
                    P.op("act", lambda: A_.activation(out=stat[:, 32 + i:33 + i], in_=stat[:, 16 + i:17 + i], func=AF.Exp, scale=-0.5),
                         reads=[("stat", 16 + i)], writes=[("stat", 32 + i)])
                    if not big:
                        P.op("dve", lambda: V_.scalar_tensor_tensor(out=hb[hs], in0=xt[s], scalar=stat[:, 32 + i:33 + i], in1=gbc,
                                                                    op0=ALU.mult, op1=ALU.mult),
                             reads=[xk[s], ("stat", 32 + i), gk], writes=[("C", "hb", hs)])
                    return (s, hs)

                def s2(st, i=i):
                    s, hs = st
                    if big:
                        P.op("dve", lambda: V_.scalar_tensor_tensor(out=hb[hs], in0=xt[s], scalar=stat[:, 32 + i:33 + i], in1=gbc,
                                                                    op0=ALU.mult, op1=ALU.mult),
                             reads=[xk[s], ("stat", 32 + i), gk], writes=[("C", "hb", hs)])
                    s = hs
                    b0 = pairs.next()
                    pT = pair(b0).bitcast(BF16)
                    for kc in range(NKC):
                        P.op("pe", (lambda kc=kc: PE.transpose(out=pT[:, kc * 128:(kc + 1) * 128], in_=hb[s][:, kc * 128:(kc + 1) * 128],
                                                                identity=ident_b[:])),
                             reads=[("C", "hb", s), "ident_b"], writes=[("ps", b0), ("ps", b0 + 1)])
                    return (b0, pT)

                def s3(st, i=i):
                    b0, pT = st
                    if big and i % 3 != 0:
                        P.op("dve", lambda: V_.tensor_copy(out=dstT[:, :, i * 128:(i + 1) * 128],
                                                           in_=pT.rearrange("p (k t) -> p k t", k=NKC)),
                             reads=[("ps", b0), ("ps", b0 + 1)], writes=[(dkey, i)])
                    else:
                        P.op("act", lambda: A_.activation(out=dstT[:, :, i * 128:(i + 1) * 128],
                                                          in_=pT.rearrange("p (k t) -> p k t", k=NKC), func=AF.Copy),
                             reads=[("ps", b0), ("ps", b0 + 1)], writes=[(dkey, i)])
                if big:
                    units.append((s1, s2, s3))
                else:
                    units.append((s1, (lambda st, s2=s2, s3=s3: s3(s2(st)))))
            pre = [pg]
            if tile0 > 0:
                def cp():
                    n0 = (ntiles - tile0) * 128
                    for hh in range(2):
                        P.op("dve", (lambda hh=hh: V_.tensor_copy(out=dstT[:, hh * 8:(hh + 1) * 8, 0:tile0 * 128],
                                                                  in_=dstT[:, hh * 8:(hh + 1) * 8, n0:n0 + tile0 * 128])),
                             reads=[(dkey, t_) for t_ in range(ntiles - tile0, ntiles)], writes=[(dkey, t_) for t_ in range(tile0)])
                pre.append(cp)
            return pre + pipeline2_pieces(units)

        def norm_units(wt, wkey, col0, src, skeys_fn, tok_segs, gain_ap, dst_fn, dkeys_fn, sq, lnb, units, lnp="A", raw=None):
            for si, (t0, n) in enumerate(tok_segs):
                def s1(si=si, t0=t0, n=n):
                    a = single.next()
                    j = wstate.setdefault("nu", 0) % len(raw)
                    wstate["nu"] += 1
                    for kc in range(NKC):
                        P.op("pe", (lambda kc=kc: PE.matmul(bank(a)[:, 0:n], lhsT=wt[:, kc, col0:col0 + 128], rhs=src[:, kc, t0:t0 + n],
                                                            start=(kc == 0), stop=(kc == NKC - 1))),
                             reads=[wkey] + skeys_fn(t0, n), writes=[("ps", a)])
                    P.op("act", lambda: A_.activation(out=sq[j][:, 0:n], in_=bank(a)[:, 0:n], func=AF.Square),
                         reads=[("ps", a)], writes=[(lnp, "sq", j)])
                    P.op("act", lambda: A_.activation(out=raw[j][:, 0:n], in_=bank(a)[:, 0:n], func=AF.Copy),
                         reads=[("ps", a)], writes=[(lnp, "raw", j)])
                    return (a, j)

                def s2(st, si=si, t0=t0, n=n):
                    a, j = st
                    b = single.next()
                    P.op("pe", lambda: PE.matmul(bank(b)[:, 0:n], lhsT=onesb[:], rhs=sq[j][:, 0:n], start=True, stop=True),
                         reads=[(lnp, "sq", j), "onesb"], writes=[("ps", b)])
                    P.op("act", lambda: A_.activation(out=lnb[j][:, 0:n], in_=bank(b)[:, 0:n], func=AF.Ln, bias=epsb[:, 0:1], scale=1.0),
                         reads=[("ps", b), "epsb"], writes=[(lnp, "ln", j)])
                    P.op("act", lambda: A_.activation(out=lnb[j][:, 0:n], in_=lnb[j][:, 0:n], func=AF.Exp, scale=-0.5),
                         reads=[(lnp, "ln", j)], writes=[(lnp, "ln", j)])
                    return st

                def s3(st, si=si, t0=t0, n=n):
                    a, j = st
                    P.op("dve", lambda: V_.scalar_tensor_tensor(out=dst_fn(si, n), in0=raw[j][:, 0:n], scalar=gain_ap, in1=lnb[j][:, 0:n],
                                                                op0=ALU.mult, op1=ALU.mult),
                         reads=[(lnp, "raw", j), (lnp, "ln", j), "vecs", "gqs"], writes=dkeys_fn(si))
                units.append((s1, s2, s3))

        def hkeys(t0, n):
            return [("hT", t) for t in range(t0 // 128, (t0 + n + 127) // 128)]

        def gate_u(wt, wkey, col0, seg, th, u_dst, ukeys):
            a = single.next()
            t0 = HALO + seg * 512
            j = wstate.setdefault("gu", 0) % 2
            wstate["gu"] += 1
            for kc in range(NKC):
                P.op("pe", (lambda kc=kc: PE.matmul(bank(a), lhsT=wt[:, kc, col0:col0 + 128], rhs=hT[:, kc, t0:t0 + 512],
                                                    start=(kc == 0), stop=(kc == NKC - 1))),
                     reads=[wkey] + hkeys(t0, 512), writes=[("ps", a)])
            P.op("act", lambda: A_.activation(out=th[j], in_=bank(a), func=AF.Tanh, scale=0.5),
                 reads=[("ps", a)], writes=[("A", "th", j)])
            P.op("dve", lambda: V_.scalar_tensor_tensor(out=u_dst, in0=th[j], scalar=1.0, in1=bank(a), op0=ALU.add, op1=ALU.mult),
                 reads=[("A", "th", j), ("ps", a)], writes=ukeys)

        interleave(phase0_pieces(xs, 0, SL // 128, gain_d, hT, "hT", big=True), [])
        interleave(phase0_pieces(mem, 0, 2, mgain_d, memT, "yp"), [])
        fence()
        ar = Arena()
        sq = [ar.bf16(512) for _ in range(3)]
        lnb = [ar.f32(512) for _ in range(3)]
        raw = [ar.f32(512) for _ in range(3)]
        MKEYS = [("yp", i) for i in range(4)]
        wt, wkey = wnext("memk")
        units = []
        for h in range(4):
            norm_units(wt, wkey, h * 128, memT, (lambda t0, n: MKEYS), [(0, 256)], vecs[:, 20 + h:21 + h],
                       (lambda si, n, h=h: mk_n[:, h, :]), (lambda si, h=h: [("mk", h)]), sq, lnb, units, raw=raw)
        pipeline2(units)
        wt, wkey = wnext("memv")
        for tt in range(2):
            a = single.next()
            for kc in range(NKC):
                P.op("pe", (lambda kc=kc, a=a, tt=tt, wt=wt: PE.matmul(bank(a), lhsT=memT[:, kc, tt * 128:(tt + 1) * 128], rhs=wt[:, kc, :],
                                                                       start=(kc == 0), stop=(kc == NKC - 1))),
                     reads=[wkey] + MKEYS, writes=[("ps", a)])
            P.op("act", (lambda a=a, tt=tt: A_.activation(out=mv[:, tt, :], in_=bank(a), func=AF.Copy)),
                 reads=[("ps", a)], writes=[("mv", tt)])

        for p in range(NPASS):

            fence()
            fence2()
            fence3()
            ar = Arena()
            ar2 = Arena(arena2)
            sets = [dict(qn=ar.bf16(T), kn=ar.bf16(SL), Vt=ar.bf16(SL), u=ar.f32(T), i=i) for i in range(2)]
            rec = [ar.f32(128) for _ in range(2)]
            tq = [ar.f32(128) for _ in range(2)]
            th = [ar.f32(512) for _ in range(2)]
            lnb = [ar2.f32(512) for _ in range(3)]
            raw = [ar2.f32(512) for _ in range(3)]
            sq = [ar2.bf16(512) for _ in range(3)]
            tmpS = [ar2.f32(640) for _ in range(3)]
            PT = [ar2.bf16(640) for _ in range(3)]

            def proj_pieces(h, S, wt, wkey):
                si_ = S["i"]
                qn, kn, Vt, u = S["qn"], S["kn"], S["Vt"], S["u"]
                bs = h % 2
                koff = 0 if p == 0 else 1

                def p0():
                    P.op("sp", lambda: SP.dma_start(out=Bh[bs][:], in_=relb_d[h, :, :]), writes=[("Bh", bs)], dma_key=("Bh", bs))
                    P.op("pool", lambda: G_.memset(Bh[bs][64:128, 0:64], NEG), reads=[("Bh", bs)], writes=[("Bh", bs)])
                    P.op("pool", lambda: G_.memset(Bh[bs][0:64, 576:640], NEG), reads=[("Bh", bs)], writes=[("Bh", bs)])
                    P.op("act", lambda: A_.activation(out=Bh[bs][:], in_=Bh[bs][:], func=AF.Exp), reads=[("Bh", bs)], writes=[("Bh", bs)])
                    if p > 0:
                        P.op("sp", lambda: SP.dma_start(out=kn[:, 0:HALO], in_=kvs_k[h, :, :]), reads=[("kvs", h)],
                             writes=[("A", "kn", si_, 0)], dma_key=("kvl", si_))
                        P.op("sp", lambda: SP.dma_start(out=Vt[:, 0:HALO], in_=kvs_v[h, :, :]), reads=[("kvs", h)],
                             writes=[("A", "V", si_, 0)], dma_key=("kvl", si_))
                units = []
                ksegs = [(0, 512), (512, 512), (1024, 512)]
                norm_units(wt, wkey, 128, hT, hkeys, ksegs if p == 0 else ksegs[1:], vecs[:, 8 + h:9 + h],
                           (lambda si, n: kn[:, (si + koff) * 512:(si + koff) * 512 + n]),
                           (lambda si: [("A", "kn", si_, si + koff)]), sq, lnb, units, lnp="B", raw=raw)
                norm_units(wt, wkey, 0, hT, hkeys, [(HALO, 512), (HALO + 512, 512)], gqs[:, h:h + 1],
                           (lambda si, n: qn[:, si * 512:si * 512 + n]), (lambda si: [("A", "qn", si_, si)]), sq, lnb, units, lnp="B", raw=raw)
                pcs = [p0] + pipeline2_pieces(units)

                def vgroup(tg):
                    a = single.next()
                    for t4 in range(4):
                        tt = tg * 4 + t4
                        for kc in range(NKC):
                            P.op("pe", (lambda kc=kc, t4=t4, tt=tt: PE.matmul(
                                bank(a)[:, t4 * 128:(t4 + 1) * 128], lhsT=hT[:, kc, tt * 128:(tt + 1) * 128], rhs=wt[:, kc, 256:384],
                                start=(kc == 0), stop=(kc == NKC - 1))),
                                reads=[wkey, ("hT", tt)], writes=[("ps", a)])
                    P.op("act", lambda: A_.activation(out=Vt[:, tg * 512:(tg + 1) * 512], in_=bank(a), func=AF.Copy),
                         reads=[("ps", a)], writes=[("A", "V", si_, tg)])
                for tg in range(koff, 3):
                    pcs.append(lambda tg=tg: vgroup(tg))
                for seg in range(2):
                    pcs.append(lambda seg=seg: gate_u(wt, wkey, 384, seg, th, u[:, seg * 512:(seg + 1) * 512], [("A", "u", si_, seg)]))
                if p + 1 < NPASS:
                    def save():
                        P.op("sp", lambda: SP.dma_start(out=kvs_k[h, :, :], in_=kn[:, T:T + HALO]), reads=[("A", "kn", si_, 2)],
                             writes=[("kvs", h)], dma_key="kvs_st")
                        P.op("sp", lambda: SP.dma_start(out=kvs_v[h, :, :], in_=Vt[:, T:T + HALO]), reads=[("A", "V", si_, 2)],
                             writes=[("kvs", h)], dma_key="kvs_st")
                    pcs.append(save)
                return pcs

            def attn_pieces(h, S):
                si_ = S["i"]
                qn, kn, Vt, u = S["qn"], S["kn"], S["Vt"], S["u"]
                bs = h % 2
                units = []
                for r in range(8):
                    def s1(r=r):
                        b0 = pairs.next()
                        pa = pair(b0)
                        i = r % 3
                        for m in range(5):
                            kt = r + 4 - m
                            P.op("pe", (lambda m=m, kt=kt: PE.matmul(pa[:, m * 128:(m + 1) * 128], lhsT=kn[:, kt * 128:(kt + 1) * 128],
                                                                      rhs=qn[:, r * 128:(r + 1) * 128], start=True, stop=True)),
                                 reads=[("A", "kn", si_, kt // 4), ("A", "qn", si_, r // 4)], writes=[("ps", b0 if m < 4 else b0 + 1)])
                        if p == 0 and r < 4:
                            c = (r + 1) * 128
                            P.op("act", lambda: A_.activation(out=tmpS[i][:, 0:c], in_=pa[:, 0:c], func=AF.Exp),
                                 reads=[("ps", b0), ("ps", b0 + 1)], writes=[("B", "tmpS", i)])
                            P.op("act", lambda: A_.activation(out=tmpS[i][:, c:640], in_=pa[:, c:640], func=AF.Exp, bias=kmask[:, 0:1], scale=1.0),
                                 reads=[("ps", b0), ("ps", b0 + 1), "kmask"], writes=[("B", "tmpS", i)])
                        else:
                            P.op("act", lambda: A_.activation(out=tmpS[i], in_=pa[:, 0:640], func=AF.Exp),
                                 reads=[("ps", b0), ("ps", b0 + 1)], writes=[("B", "tmpS", i)])
                        return (b0, i)

                    def s2(st, r=r):
                        b0, i = st
                        P.op("pool", lambda: G_.tensor_tensor(out=PT[i], in0=tmpS[i], in1=Bh[bs][:], op=ALU.mult),
                             reads=[("B", "tmpS", i), ("Bh", bs)], writes=[("B", "PT", i)])

                    def s3(st, r=r):
                        b0, i = st
                        ri = r % 2
                        c = single.next()
                        pc = bank(c)
                        for m in range(5):
                            kt = r + 4 - m
                            P.op("pe", (lambda m=m, kt=kt: PE.matmul(pc[:, 0:128], lhsT=Vt[:, kt * 128:(kt + 1) * 128],
                                                                      rhs=PT[i][:, m * 128:(m + 1) * 128], start=(m == 0), stop=(m == 4))),
                                 reads=[("A", "V", si_, kt // 4), ("B", "PT", i)], writes=[("ps", c)])
                        for m in range(5):
                            P.op("pe", (lambda m=m: PE.matmul(pc[:, 128:256], lhsT=ones1[:], rhs=PT[i][:, m * 128:(m + 1) * 128],
                                                               start=(m == 0), stop=(m == 4))),
                                 reads=["ones1", ("B", "PT", i)], writes=[("ps", c)])
                        P.op("dve", lambda: V_.reciprocal(out=rec[ri], in_=pc[:, 128:256]), reads=[("ps", c)], writes=[("A", "rec", ri)])
                        P.op("dve", lambda: V_.tensor_tensor(out=tq[ri], in0=pc[:, 0:128], in1=rec[ri], op=ALU.mult),
                             reads=[("ps", c), ("A", "rec", ri)], writes=[("A", "tq", ri)])
                        P.op("dve", lambda: V_.scalar_tensor_tensor(out=o_sb[:, h, r * 128:(r + 1) * 128], in0=tq[ri], scalar=0.5,
                                                                    in1=u[:, r * 128:(r + 1) * 128], op0=ALU.mult, op1=ALU.mult),
                             reads=[("A", "tq", ri), ("A", "u", si_, r // 4)], writes=[("o", h, r // 4)])
                    units.append((s1, s2, s3))
                return pipeline2_pieces(units)

            prev_attn = []
            for h in range(8):
                wt, wkey = wnext("head%d" % h)
                interleave(prev_attn, proj_pieces(h, sets[h % 2], wt, wkey))
                prev_attn = attn_pieces(h, sets[h % 2])
            interleave(prev_attn, [])

            fence()
            ar = Arena()
            va = ar.f32(1152)
            X = ar.f32(1152)
            Y = ar.f32(1152)
            pooled = [ar.bf16(T) for _ in range(4)]
            th = [ar.f32(512) for _ in range(2)]
            ub = [ar.f32(512) for _ in range(2)]
            wt, wkey = wnext("vb")
            for g in range(4):
                for (t0, n, c0) in ((384, 128, 0), (512, 512, 128), (1024, 512, 640)):
                    a = single.next()
                    for kc in range(NKC):
                        P.op("pe", (lambda kc=kc, a=a, t0=t0, n=n, g=g, wt=wt: PE.matmul(
                            bank(a)[:, 0:n], lhsT=wt[:, kc, g * 128:(g + 1) * 128], rhs=hT[:, kc, t0:t0 + n],
                            start=(kc == 0), stop=(kc == NKC - 1))),
                            reads=[wkey] + hkeys(t0, n), writes=[("ps", a)])
                    P.op("act", (lambda a=a, n=n, c0=c0: A_.activation(out=va[:, c0:c0 + n], in_=bank(a)[:, 0:n], func=AF.Copy)),
                         reads=[("ps", a)], writes=[("A", "va")])
                srcb, k, sh = va, "va", 1
                bufs = [(X, "X"), (Y, "Y")]
                for lvl in range(g + 1):
                    dstb, dk = bufs[lvl % 2]
                    lo = 2 * sh - 1
                    P.op("pool", (lambda dstb=dstb, srcb=srcb, lo=lo, sh=sh: G_.tensor_tensor(
                        out=dstb[:, lo:1152], in0=srcb[:, lo:1152], in1=srcb[:, lo - sh:1152 - sh], op=ALU.add)),
                        reads=[("A", k)], writes=[("A", dk)])
                    srcb, k, sh = dstb, dk, sh * 2
                if p == 0:
                    P.op("pool", (lambda srcb=srcb, g=g: G_.tensor_tensor(out=srcb[:, 128:144], in0=srcb[:, 128:144],
                                                                          in1=pcorr[:, g * 16:(g + 1) * 16], op=ALU.mult)),
                         reads=[("A", k), "pcorr"], writes=[("A", k)])
                P.op("dve", (lambda srcb=srcb, g=g: V_.scalar_tensor_tensor(out=pooled[g], in0=srcb[:, 128:1152], scalar=1.0 / POOL_W[g],
                                                                            in1=va[:, 128:1152], op0=ALU.mult, op1=ALU.subtract)),
                     reads=[("A", k), ("A", "va")], writes=[("A", "pooled", g)])
            wt, wkey = wnext("gb")
            for g in range(4):
                for seg in range(2):
                    j = wstate.setdefault("gu", 0) % 2
                    gate_u(wt, wkey, g * 128, seg, th, ub[j], [("A", "ub", j)])
                    a = single.next()
                    P.op("pe", (lambda a=a, g=g, seg=seg: PE.matmul(bank(a), lhsT=poolw[:, g, :], rhs=pooled[g][:, seg * 512:(seg + 1) * 512],
                                                                    start=True, stop=True)),
                         reads=["poolw", ("A", "pooled", g)], writes=[("ps", a)])
                    P.op("dve", (lambda a=a, g=g, seg=seg, j=j: V_.scalar_tensor_tensor(
                        out=o_sb[:, 8 + g, seg * 512:(seg + 1) * 512], in0=bank(a), scalar=psc[:, g:g + 1], in1=ub[j],
                        op0=ALU.mult, op1=ALU.mult)),
                        reads=[("ps", a), "psc", ("A", "ub", j)], writes=[("o", 8 + g, seg)])

            fence()
            fence2()
            ar = Arena()
            ar2 = Arena(arena2)
            qn4 = [ar.bf16(T) for _ in range(4)]
            th = [ar.f32(512) for _ in range(2)]
            ub = [ar.f32(512) for _ in range(2)]
            PT2 = [ar.bf16(1024) for _ in range(2)]
            rec2 = [ar.f32(512) for _ in range(2)]
            sq = [ar2.bf16(512) for _ in range(3)]
            lnb = [ar2.f32(512) for _ in range(3)]
            raw = [ar2.f32(512) for _ in range(3)]
            wt, wkey = wnext("qm")
            units = []
            for h in range(4):
                norm_units(wt, wkey, h * 128, hT, hkeys, [(HALO, 512), (HALO + 512, 512)], gqs[:, 8 + h:9 + h],
                           (lambda si, n, h=h: qn4[h][:, si * 512:si * 512 + n]), (lambda si, h=h: [("A", "qn4", h, si)]), sq, lnb, units, lnp="B", raw=raw)
            pipeline2(units)
            wt, wkey = wnext("gm")
            units = []
            for h in range(4):
                for seg in range(2):
                    def s1(h=h, seg=seg, wt=wt, wkey=wkey):
                        j = wstate.setdefault("gu", 0) % 2
                        gate_u(wt, wkey, h * 128, seg, th, ub[j], [("A", "ub", j)])
                        b0 = pairs.next()
                        pa = pair(b0)
                        for jj in range(2):
                            P.op("pe", (lambda jj=jj: PE.matmul(pa[:, jj * 512:(jj + 1) * 512], lhsT=mk_n[:, h, jj * 128:(jj + 1) * 128],
                                                                rhs=qn4[h][:, seg * 512:(seg + 1) * 512], start=True, stop=True)),
                                 reads=[("mk", h), ("A", "qn4", h, seg)], writes=[("ps", b0 + jj)])
                        P.op("act", lambda: A_.activation(out=PT2[j], in_=pa, func=AF.Exp),
                             reads=[("ps", b0), ("ps", b0 + 1)], writes=[("A", "PT2", j)])
                        return j

                    def s2(j, h=h, seg=seg):
                        c = single.next()
                        for jj in range(2):
                            P.op("pe", (lambda jj=jj: PE.matmul(bank(c), lhsT=mv[:, jj, h * 128:(h + 1) * 128],
                                                                rhs=PT2[j][:, jj * 512:(jj + 1) * 512], start=(jj == 0), stop=(jj == 1))),
                                 reads=[("mv", jj), ("A", "PT2", j)], writes=[("ps", c)])
                        d = single.next()
                        for jj in range(2):
                            P.op("pe", (lambda jj=jj: PE.matmul(bank(d), lhsT=ones2[:], rhs=PT2[j][:, jj * 512:(jj + 1) * 512],
                                                                start=(jj == 0), stop=(jj == 1))),
                                 reads=["ones2", ("A", "PT2", j)], writes=[("ps", d)])
                        P.op("dve", lambda: V_.reciprocal(out=rec2[j], in_=bank(d)), reads=[("ps", d)], writes=[("A", "rec2", j)])
                        P.op("dve", lambda: V_.tensor_tensor(out=rec2[j], in0=bank(c), in1=rec2[j], op=ALU.mult),
                             reads=[("ps", c), ("A", "rec2", j)], writes=[("A", "rec2", j)])
                        P.op("pool", lambda: G_.tensor_tensor(out=o_sb[:, 12 + h, seg * 512:(seg + 1) * 512], in0=rec2[j], in1=ub[j], op=ALU.mult),
                             reads=[("A", "rec2", j), ("A", "ub", j)], writes=[("o", 12 + h, seg)])
                    units.append((s1, s2))
            pipeline2(units)

            fence()
            fence2()
            ar = Arena()
            thg = [[ar.f32(512) for _ in range(3)] for _ in range(2)]
            acc = [ar.f32(512) for _ in range(3)]
            ssum = ar.f32(512)
            xres = [ar.f32(512) for _ in range(3)]
            res = [ar.f32(512) for _ in range(2)]
            allb = Rot([4, 5, 6, 7, 0, 1, 2, 3])
            KB = [(0, 8), (8, 12), (12, 16)]
            for db in range(16):
                wt, wkey = wnext("mg%d" % db)
                for half in range(2):
                    j = (db * 2 + half) % 2
                    t0 = HALO + half * 512
                    zb = []
                    for br in range(3):
                        a = allb.next()
                        zb.append(a)
                        for kc in range(NKC):
                            P.op("pe", (lambda kc=kc, a=a, br=br, wt=wt, t0=t0: PE.matmul(
                                bank(a), lhsT=wt[:, kc, br * 128:(br + 1) * 128], rhs=hT[:, kc, t0:t0 + 512],
                                start=(kc == 0), stop=(kc == NKC - 1))),
                                reads=[wkey] + hkeys(t0, 512), writes=[("ps", a)])
                        P.op("act", (lambda a=a, br=br, j=j, db=db: A_.activation(out=thg[j][br], in_=bank(a), func=AF.Tanh,
                                                                                  bias=bmh[:, br * 16 + db:br * 16 + db + 1], scale=0.5)),
                             reads=[("ps", a), "bmh"], writes=[("A", "thg", j, br)])
                    for br in range(3):
                        a = allb.next()
                        k0, k1 = KB[br]
                        for kc in range(k0, k1):
                            P.op("pe", (lambda kc=kc, a=a, wt=wt, half=half, k0=k0, k1=k1: PE.matmul(
                                bank(a), lhsT=wt[:, kc, 384:512], rhs=o_sb[:, kc, half * 512:(half + 1) * 512],
                                start=(kc == k0), stop=(kc == k1 - 1))),
                                reads=[wkey, ("o", kc, half)], writes=[("ps", a)])
                        P.op("dve", (lambda a=a, br=br, j=j: V_.scalar_tensor_tensor(out=acc[br], in0=thg[j][br], scalar=1.0, in1=bank(a),
                                                                                     op0=ALU.add, op1=ALU.mult)),
                             reads=[("A", "thg", j, br), ("ps", a)], writes=[("A", "acc", br)])
                    P.op("pool", lambda: G_.tensor_tensor(out=ssum, in0=acc[0], in1=acc[1], op=ALU.add),
                         reads=[("A", "acc", 0), ("A", "acc", 1)], writes=[("A", "ssum")])
                    P.op("pool", (lambda db=db, half=half: G_.tensor_tensor(out=yp[:, db, half * 512:(half + 1) * 512], in0=ssum, in1=acc[2], op=ALU.add)),
                         reads=[("A", "ssum"), ("A", "acc", 2)], writes=[("yp", db)])

            def xload(un):
                c_, tt_ = un // 8, un % 8
                jx = un % 3
                r0 = HALO + p * T + tt_ * 128
                P.op("act", lambda: A_.dma_start(out=xres[jx], in_=xs[r0:r0 + 128, c_ * 512:(c_ + 1) * 512]),
                     writes=[("A", "xres", jx)], dma_key=("xres", jx))
            xload(0)
            p3 = []
            wcur = {}
            for c in range(4):
                for tt in range(8):
                    def unit(c=c, tt=tt):
                        if tt == 0:
                            wcur["w"] = wnext("wo%d" % c)
                        wt, wkey = wcur["w"]
                        un = c * 8 + tt
                        j = un % 2
                        jx = un % 3
                        a = allb.next() if p + 1 == NPASS else single.next()
                        if un + 1 < 32:
                            xload(un + 1)
                        for kc in range(NKC):
                            P.op("pe", (lambda kc=kc: PE.matmul(bank(a), lhsT=yp[:, kc, tt * 128:(tt + 1) * 128], rhs=wt[:, kc, :],
                                                                start=(kc == 0), stop=(kc == NKC - 1))),
                                 reads=[wkey, ("yp", kc)], writes=[("ps", a)])
                        P.op("dve", lambda: V_.scalar_tensor_tensor(out=res[j], in0=bank(a), scalar=0.5, in1=xres[jx],
                                                                    op0=ALU.mult, op1=ALU.add),
                             reads=[("ps", a), ("A", "xres", jx)], writes=[("A", "res", j)])
                        orow = p * T + tt * 128
                        P.op("sp", lambda: SP.dma_start(out=out_d[orow:orow + 128, c * 512:(c + 1) * 512], in_=res[j]),
                             reads=[("A", "res", j)], dma_key=("out", j))
                    p3.append(unit)
            if p + 1 < NPASS:
                interleave(p3, phase0_pieces(xs, (p + 1) * T, SL // 128, gain_d, hT, "hT", tile0=HALO // 128, xq="act"))
            else:
                interleave(p3, [])

        P.emit()
    return nc


_CACHE = {}


def _program():
    if "nc" not in _CACHE:
        _CACHE["nc"] = build_program()
    return _CACHE["nc"]


def kernel(x, mem, norm_gain, mem_norm_gain, w_in, w_merge, b_merge, a_q_gain, a_k_gain,
           a_rel_bias, pool_w, pool_scale, w_mem_kv, m_q_gain, m_k_gain,
           w_branch_a, w_branch_b, w_branch_m, w_out):
    f = lambda a: np.ascontiguousarray(np.asarray(a, dtype=np.float32))
    x = f(x); mem = f(mem)
    B, S, _ = x.shape
    half_len = S // 2
    gain_bc = f(np.broadcast_to(np.asarray(norm_gain)[0][None, :], (128, D)))
    mgain_bc = f(np.broadcast_to(np.asarray(mem_norm_gain)[0][None, :], (128, D)))
    vecs = f(np.concatenate([
        np.asarray(a_q_gain)[0].T, np.asarray(a_k_gain)[0].T, np.asarray(m_q_gain)[0].T, np.asarray(m_k_gain)[0].T,
        np.asarray(pool_scale)[0].reshape(4, 128).T, np.asarray(b_merge)[0].reshape(48, 128).T], axis=1))
    ii = np.arange(128)[:, None]
    jj = np.arange(640)[None, :]
    idx = np.clip(jj - ii, -256, 256) + 256
    relb = f(np.asarray(a_rel_bias)[0][:, idx])
    ident = np.eye(128, dtype=np.float32)
    wstream = build_wstream(dict(
        w_in=np.asarray(w_in)[0], w_merge=np.asarray(w_merge)[0], w_mem_kv=np.asarray(w_mem_kv)[0],
        w_a=np.asarray(w_branch_a)[0], w_b=np.asarray(w_branch_b)[0], w_m=np.asarray(w_branch_m)[0],
        w_out=np.asarray(w_out)[0]))
    shared = dict(gain_bc=gain_bc, mgain_bc=mgain_bc, wstream=wstream, vecs=vecs, relb=relb,
                  pool_w=f(np.asarray(pool_w)[0]), ident=ident)
    in_maps = []
    tt = np.arange(16)
    for c in range(N_CORES):
        b, hf = c // 2, c % 2
        xs = np.zeros((HALO + half_len, D), np.float32)
        if hf == 0:
            xs[HALO:] = x[b, 0:half_len]
        else:
            xs[:] = x[b, half_len - HALO:S]
        km = np.full((128, 1), NEG if hf == 0 else 0.0, np.float32)
        pc = np.ones((128, 64), np.float32)
        if hf == 0:
            for g, w in enumerate(POOL_W):
                pc[:, g * 16:(g + 1) * 16] = (w / np.minimum(tt + 1, w)).astype(np.float32)[None, :]
        m = dict(shared)
        m.update(xs=xs, mem=mem[b], kmask=km, pcorr=pc)
        in_maps.append(m)
    nc = _program()
    r = run_bass_kernel_spmd(nc, in_maps, core_ids=list(range(N_CORES)))
    out = np.empty((B, S, D), np.float32)
    for c in range(N_CORES):
        b, hf = c // 2, c % 2
        out[b, hf * half_len:(hf + 1) * half_len] = r.results[c]["out"]
    return out
```
